# Optimizing a Trainium2 kernel written in Bass

```python
import jax
import jax.numpy as jnp
from jax import lax
import numpy as np

D_MODEL = 1024
BATCH = 8
SEQ = 4096
DEPTH = 1

HEAD_DIM = 64
ATTN_WIDTH = D_MODEL // 2
N_ATTN_HEADS = ATTN_WIDTH // HEAD_DIM
N_KV_HEADS = 2
KV_WIDTH = N_KV_HEADS * HEAD_DIM
WINDOW = 128
BLOCK = 128
ROPE_THETA = 500000.0
ROTARY_DIM = HEAD_DIM // 4
POOL_WINDOWS = (2, 4, 8, 16)
N_POOL_GROUPS = len(POOL_WINDOWS)
POOL_WIDTH = D_MODEL // 2
POOL_GROUP_WIDTH = POOL_WIDTH // N_POOL_GROUPS
MIX_WIDTH = ATTN_WIDTH + POOL_WIDTH
IN_WIDTH = ATTN_WIDTH + 2 * KV_WIDTH + POOL_WIDTH
D_FF = 2816
EPS = 1e-6

kernel_name = 'hybrid_window_gqa_multiscale_pool_macaron'


def rms_norm(x, g):
    xf = x.astype(jnp.float32)
    y = xf * lax.rsqrt(jnp.mean(xf * xf, axis=-1, keepdims=True) + EPS)
    return (y * g.astype(jnp.float32)).astype(x.dtype)


def swiglu(x, w_gate, w_up, w_down):
    return (jax.nn.silu(x @ w_gate) * (x @ w_up)) @ w_down


def rope_tables(positions):
    inv_freq = ROPE_THETA ** (-jnp.arange(0, ROTARY_DIM, 2, dtype=jnp.float32) / ROTARY_DIM)
    ang = positions.astype(jnp.float32)[:, None] * inv_freq[None, :]
    emb = jnp.concatenate([ang, ang], axis=-1)
    return jnp.cos(emb)[None, :, None, :], jnp.sin(emb)[None, :, None, :]


def apply_partial_rope(t, cos, sin):
    tf = t.astype(jnp.float32)
    rot, rest = tf[..., :ROTARY_DIM], tf[..., ROTARY_DIM:]
    half = ROTARY_DIM // 2
    rot_half = jnp.concatenate([-rot[..., half:], rot[..., :half]], axis=-1)
    rot = rot * cos + rot_half * sin
    return jnp.concatenate([rot, rest], axis=-1).astype(t.dtype)


def window_attention(q, k, v, sink):
    B, S, H, D = q.shape
    nb = S // BLOCK
    G = H // N_KV_HEADS
    qb = q.reshape(B, nb, BLOCK, N_KV_HEADS, G, D)

    def bands(t):
        tp = jnp.pad(t, ((0, 0), (BLOCK, BLOCK), (0, 0), (0, 0)))
        tb = tp.reshape(B, nb + 2, BLOCK, N_KV_HEADS, D)
        return jnp.concatenate([tb[:, :-2], tb[:, 1:-1], tb[:, 2:]], axis=2)

    kb, vb = bands(k), bands(v)
    scores = jnp.einsum('bnqkgd,bnskd->bnkgqs', qb, kb,
                        preferred_element_type=jnp.float32) * (D ** -0.5)
    qpos = jnp.arange(nb)[:, None] * BLOCK + jnp.arange(BLOCK)[None, :]
    kpos = jnp.arange(nb)[:, None] * BLOCK + jnp.arange(3 * BLOCK)[None, :] - BLOCK
    dist = qpos[:, :, None] - kpos[:, None, :]
    valid = (jnp.abs(dist) <= WINDOW) & (kpos[:, None, :] >= 0) & (kpos[:, None, :] < S)
    scores = jnp.where(valid[None, :, None, None], scores, -1e30)
    sink_l = sink.astype(jnp.float32).reshape(N_KV_HEADS, G)[None, None, :, :, None, None]
    m = jnp.maximum(jnp.max(scores, axis=-1, keepdims=True), sink_l)
    p = jnp.exp(scores - m)
    denom = jnp.sum(p, axis=-1, keepdims=True) + jnp.exp(sink_l - m)
    p = (p / denom).astype(v.dtype)
    out = jnp.einsum('bnkgqs,bnskd->bnqkgd', p, vb)
    return out.reshape(B, S, H * D)


def multiscale_pool(u, w_pool, pool_scale):
    B, S, _ = u.shape
    uf = u.astype(jnp.float32)
    csum = jnp.concatenate([jnp.zeros((B, 1, POOL_WIDTH), jnp.float32),
                            jnp.cumsum(uf, axis=1)], axis=1)
    t = jnp.arange(S)
    means = []
    for gi, w in enumerate(POOL_WINDOWS):
        cg = csum[..., gi * POOL_GROUP_WIDTH:(gi + 1) * POOL_GROUP_WIDTH]
        half = w // 2

        def win_mean(lo, hi):
            a = jnp.clip(lo, 0, S)
            b = jnp.clip(hi + 1, 0, S)
            s = jnp.take(cg, b, axis=1) - jnp.take(cg, a, axis=1)
            return s / (b - a).astype(jnp.float32)[None, :, None]

        means.append(0.5 * (win_mean(t - half, t + half - 1) + win_mean(t - half + 1, t + half)))
    mean = jnp.concatenate(means, axis=-1)
    d = (mean - uf).reshape(B, S, N_POOL_GROUPS, POOL_GROUP_WIDTH)
    y = jnp.einsum('bsgc,gcd->bsgd', d, w_pool.astype(jnp.float32)).reshape(B, S, POOL_WIDTH)
    return (y * pool_scale.astype(jnp.float32)).astype(u.dtype)


def setup_inputs(seed: int = 0) -> dict:
    key = jax.random.key(seed)
    ks = jax.random.split(key, 18)
    f32 = jnp.float32
    L = DEPTH

    def nrm(k, shape, scale):
        return jax.random.normal(k, shape, f32) * scale

    def gain(k, shape, s=0.05):
        return 1.0 + s * jax.random.normal(k, shape, f32)

    return {
        'x': nrm(ks[0], (BATCH, SEQ, D_MODEL), 1.0),
        'ffn1_norm': gain(ks[1], (L, D_MODEL)),
        'ffn1_w_gate': nrm(ks[2], (L, D_MODEL, D_FF), D_MODEL ** -0.5),
        'ffn1_w_up': nrm(ks[3], (L, D_MODEL, D_FF), D_MODEL ** -0.5),
        'ffn1_w_down': nrm(ks[4], (L, D_FF, D_MODEL), D_FF ** -0.5),
        'mix_norm': gain(ks[5], (L, D_MODEL)),
        'w_in': nrm(ks[6], (L, D_MODEL, IN_WIDTH), D_MODEL ** -0.5),
        'sink_logits': nrm(ks[7], (L, N_ATTN_HEADS), 0.5),
        'pool_w': nrm(ks[8], (L, N_POOL_GROUPS, POOL_GROUP_WIDTH, POOL_GROUP_WIDTH), POOL_GROUP_WIDTH ** -0.5),
        'pool_scale': gain(ks[9], (L, POOL_WIDTH), 0.1),
        'w_out': nrm(ks[10], (L, MIX_WIDTH, D_MODEL), MIX_WIDTH ** -0.5),
        'ffn2_norm': gain(ks[11], (L, D_MODEL)),
        'ffn2_w_gate': nrm(ks[12], (L, D_MODEL, D_FF), D_MODEL ** -0.5),
        'ffn2_w_up': nrm(ks[13], (L, D_MODEL, D_FF), D_MODEL ** -0.5),
        'ffn2_w_down': nrm(ks[14], (L, D_FF, D_MODEL), D_FF ** -0.5),
        'final_norm': gain(ks[15], (D_MODEL,)),
    }


def reference(x, ffn1_norm, ffn1_w_gate, ffn1_w_up, ffn1_w_down, mix_norm, w_in,
              sink_logits, pool_w, pool_scale, w_out, ffn2_norm, ffn2_w_gate,
              ffn2_w_up, ffn2_w_down, final_norm):
    B, S, _ = x.shape
    positions = jnp.arange(S, dtype=jnp.int32)
    cos, sin = rope_tables(positions)
    h = x
    for l in range(DEPTH):
        h = h + 0.5 * swiglu(rms_norm(h, ffn1_norm[l]), ffn1_w_gate[l], ffn1_w_up[l], ffn1_w_down[l])
        u = rms_norm(h, mix_norm[l]) @ w_in[l]
        q, k, v, pc = jnp.split(u, [ATTN_WIDTH, ATTN_WIDTH + KV_WIDTH, ATTN_WIDTH + 2 * KV_WIDTH], axis=-1)
        q = apply_partial_rope(q.reshape(B, S, N_ATTN_HEADS, HEAD_DIM), cos, sin)
        k = apply_partial_rope(k.reshape(B, S, N_KV_HEADS, HEAD_DIM), cos, sin)
        v = v.reshape(B, S, N_KV_HEADS, HEAD_DIM)
        a = window_attention(q, k, v, sink_logits[l])
        p = multiscale_pool(pc, pool_w[l], pool_scale[l])
        h = h + jnp.concatenate([a, p], axis=-1) @ w_out[l]
        h = h + 0.5 * swiglu(rms_norm(h, ffn2_norm[l]), ffn2_w_gate[l], ffn2_w_up[l], ffn2_w_down[l])
    return rms_norm(h, final_norm)
```

```python
import numpy as np
from contextlib import ExitStack

import concourse.bass as bass
import concourse.mybir as mybir
from concourse.bass_utils import run_bass_kernel_spmd

F32 = mybir.dt.float32
BF16 = mybir.dt.bfloat16
AF = mybir.ActivationFunctionType
ALU = mybir.AluOpType

S = 4096
D = 1024
DFF = 2816
NT_ALL = 32
NCH = 22
BLOCKS = [list(range(0, 6)), list(range(6, 12)), list(range(12, 17)), list(range(17, 22))]
NPASS = 4
TPP = 8
HS = 10
NEG = -30000.0


def bc(ap, axis, n):
    shp = list(ap.shape)
    a = ap.unsqueeze(axis)
    shp.insert(axis, n)
    return a.broadcast_to(shp)


class Plan:
    ENGS = ("pe", "act", "dve", "pool", "sp")

    def __init__(self):
        self.ops = {e: [] for e in self.ENGS}
        self.cnt = {}
        self.seen = {e: {} for e in self.ENGS}

    def _filter(self, eng, waits):
        w = []
        for t in waits:
            if t is None:
                continue
            if isinstance(t, list):
                for tt in t:
                    waits.append(tt)
                continue
            key, val = t
            if self.seen[eng].get(key, 0) >= val:
                continue
            self.seen[eng][key] = val
            w.append((key, val))
        return w

    def op(self, eng, fn, waits=(), inc=True):
        w = self._filter(eng, list(waits))
        tok = None
        incinfo = None
        if inc:
            self.cnt[eng] = self.cnt.get(eng, 0) + 1
            tok = (eng, self.cnt[eng])
            incinfo = (eng, 1)
        self.ops[eng].append((w, fn, incinfo))
        return tok

    def dma(self, eng, out, in_, key, waits=()):
        w = self._filter(eng, list(waits))
        self.cnt[key] = self.cnt.get(key, 0) + 16
        tok = (key, self.cnt[key])
        self.ops[eng].append((w, lambda e, o=out, i=in_: e.dma_start(out=o, in_=i), (key, 16)))
        return tok

    def wait_only(self, eng, waits):
        w = self._filter(eng, list(waits))
        if w:
            self.ops[eng].append((w, None, None))

    def emit(self, eng, e, sems):
        for (w, fn, incinfo) in self.ops[eng]:
            for (key, val) in w:
                e.wait_ge(sems[key], val)
            if fn is None:
                continue
            ins = fn(e)
            if incinfo is not None:
                ins.then_inc(sems[incinfo[0]], incinfo[1])


def build_program(S=S):
    NT_ALL = S // 128
    NPASS = NT_ALL // TPP
    nc = bass.Bass("TRN2", target_bir_lowering=False)
    dt = nc.dram_tensor
    x_d = dt("x", [S, D], F32, kind="ExternalInput").ap()
    wgu_d = [dt("wgu1", [NCH, 128, 2 * 8 * 128], F32, kind="ExternalInput").ap(),
             dt("wgu2", [NCH, 128, 2 * 8 * 128], F32, kind="ExternalInput").ap()]
    wd_d = [dt("wd1", [DFF, D], F32, kind="ExternalInput").ap(),
            dt("wd2", [DFF, D], F32, kind="ExternalInput").ap()]
    win_d = dt("win", [D, 1280], F32, kind="ExternalInput").ap()
    wout_d = dt("wout", [D, D], F32, kind="ExternalInput").ap()
    wpool_d = dt("wpool", [128, 4, 128], F32, kind="ExternalInput").ap()
    ident_d = dt("ident", [128, 128], F32, kind="ExternalInput").ap()
    negm_d = dt("negm", [128, 2, 512], F32, kind="ExternalInput").ap()
    bpool_d = dt("bpool", [128, 20, 128], F32, kind="ExternalInput").ap()
    cos_d = dt("cosT", [128, NT_ALL, 16], F32, kind="ExternalInput").ap()
    sin_d = dt("sinT", [128, NT_ALL, 16], F32, kind="ExternalInput").ap()
    gfm_d = dt("gfm", [128, 3, 8], F32, kind="ExternalInput").ap()
    gfin_d = dt("gfin", [128, D], F32, kind="ExternalInput").ap()
    pscale_d = dt("pscale", [128, 4], F32, kind="ExternalInput").ap()
    sink_d = dt("sinkb", [128, 4], F32, kind="ExternalInput").ap()
    out_d = dt("out", [S, D], F32, kind="ExternalOutput").ap()

    with ExitStack() as es:
        E = es.enter_context
        sb = lambda name, shape, dty: E(nc.sbuf_tensor(name, shape, dty))
        H = sb("H", [128, HS, D], F32)
        XNT = sb("XNT", [128, 8, 9 * 128], BF16)
        HID = sb("HID", [128, 6, 9 * 128], BF16)
        WD = sb("WD", [128, 6, D], BF16)
        WGU = sb("WGU", [128, 4, 2 * 8 * 128], BF16)
        SG = sb("SG", [128, 2, 512], BF16)
        JUNK = sb("JUNK", [128, D], BF16)
        JUNK2 = sb("JUNK2", [128, D], BF16)
        XN = sb("XN", [128, 4, D], BF16)
        EPS = sb("EPS", [128, 1], F32)
        ST = sb("ST", [128, 4 * 4 * NT_ALL + 64], F32)
        WIN = sb("WIN", [128, 8, 1280], BF16)
        WOUT = sb("WOUT", [128, 8, D], BF16)
        WPOOL = sb("WPOOL", [128, 4, 128], BF16)
        IDB = sb("IDB", [128, 128], BF16)
        NEGM = sb("NEGM", [128, 2, 512], BF16)
        BPOOL = sb("BPOOL", [128, 20, 128], BF16)
        COS = sb("COS", [128, NT_ALL, 16], F32)
        SIN = sb("SIN", [128, NT_ALL, 16], F32)
        GFM = sb("GFM", [128, 3, 8], F32)
        GFIN = sb("GFIN", [128, D], F32)
        PSCALE = sb("PSCALE", [128, 4], F32)
        SINKB = sb("SINKB", [128, 4], F32)
        ESINK = sb("ESINK", [128, 4], F32)
        ESROW = sb("ESROW", [1, 2, 512], BF16)
        ONEROW = sb("ONEROW", [1, 2, 128], BF16)
        U = sb("U", [128, 4, 1280], BF16)
        VAUG = sb("VAUG", [128, 4, 2, 128], BF16)
        QT = sb("QT", [128, 3, 4, 128], BF16)
        KT = sb("KT", [128, 4, 128], BF16)
        XNTB = sb("XNTB", [128, 2, 8, 128], BF16)
        MIXT = sb("MIXT", [128, 2, 8, 128], BF16)
        RDEN = sb("RDEN", [128, 512], F32)
        DT = sb("DT", [128, 4, 128], BF16)
        KRAW = sb("KRAW", [128, 2, 128], F32)
        QRAW = sb("QRAW", [128, 2, 512], F32)
        RT = sb("RT", [128, 2, 10, 16], F32)
        RT2 = sb("RT2", [128, 2, 10, 16], F32)
        PS = [E(nc.psum_tensor(f"ps{i}", [128, 512], F32)) for i in range(7)]
        PST = E(nc.psum_tensor("pst", [128, 8, 128], BF16))
        PSTF = PST[:].rearrange("p a b -> p (a b)").bitcast(F32)
        PT = HID[:].rearrange("p a b -> p (a b)")[:, 0:6144].rearrange("p (s k b n) -> p s k b n", s=2, k=2, b=3)

        P = Plan()
        sem_keys = ["pe", "act", "dve", "pool", "c", "cg", "c2", "cg2", "wd", "wB"] + [f"x{i}" for i in range(HS)] + \
                   [f"o{i}" for i in range(HS)] + [f"wgu{i}" for i in range(4)]
        sems = {k: E(nc.semaphore(k)) for k in sem_keys}

        bank_free = {i: [] for i in range(7)}
        bank_free["T"] = []
        h_ready = {}
        store_tok = {}
        state = {"gu": 0, "pair": 0, "sgs": 0, "dn": 0, "xn": 0, "xc": 0, "xa": 0, "st": 0}
        gu_free = [None] * 4
        sg_free = [None] * 2
        xn_free = [None] * 4
        wd_free = [None]

        P.dma("sp", GFM[:], gfm_d, "c")
        P.dma("sp", PSCALE[:], pscale_d, "c")
        c_sp = P.dma("sp", SINKB[:], sink_d, "c")
        tok_idb = P.dma("pool", IDB[:], ident_d, "cg")
        c_tok = [c_sp, tok_idb]
        late = {}

        def late_sp():
            P.dma("sp", COS[:], cos_d, "c2")
            P.dma("sp", SIN[:], sin_d, "c2")
            c_tok.append(P.dma("sp", GFIN[:], gfin_d, "c2"))

        def late_pool(b):
            if b == 1:
                P.dma("pool", WIN[:], win_d.rearrange("(k p) n -> p k n", p=128), "wB")
                late["b1"] = True
            elif b == 2:
                P.dma("pool", WOUT[:], wout_d.rearrange("(k p) n -> p k n", p=128), "wB")
                late["b2"] = True
            elif b == 3:
                P.dma("pool", NEGM[:], negm_d, "cg2")
                c_tok.append(P.dma("pool", BPOOL[:], bpool_d, "cg2"))
                late["wB"] = P.dma("pool", WPOOL[:], wpool_d, "wB")

        t_init = P.op("dve", lambda e: e.memset(ST[:], 0.0))
        t_init = P.op("dve", lambda e: e.memset(EPS[:], 1e-6))
        t_init = P.op("dve", lambda e: e.memset(VAUG[:], 1.0))
        t_es = P.op("act", lambda e: e.activation(out=ESINK[:], in_=SINKB[:], func=AF.Exp), waits=[c_tok])
        t_es1 = P.op("act", lambda e: e.activation(out=ESROW[0:1, 0, :].rearrange("p (h q) -> p h q", h=4),
                                                   in_=bc(ESINK[0:1, :], 2, 128), func=AF.Copy), waits=[t_es])
        t_es = P.op("act", lambda e: e.activation(out=ESROW[0:1, 1, :].rearrange("p (h q) -> p h q", h=4),
                                                  in_=bc(ESINK[64:65, :], 2, 128), func=AF.Copy), waits=[t_es])
        P.op("dve", lambda e: e.memset(ONEROW[0:1, 0, 0:64], 0.0))
        P.op("dve", lambda e: e.memset(ONEROW[0:1, 0, 64:128], 1.0))
        P.op("dve", lambda e: e.memset(ONEROW[0:1, 1, 0:64], 1.0))
        t_one = P.op("dve", lambda e: e.memset(ONEROW[0:1, 1, 64:128], 0.0))
        t_es = [t_es, t_one]

        def load_x(t):
            s = t % HS
            w = [store_tok.get(t - HS)]
            h_ready[t] = P.dma("sp", H[:, s, :], x_d[t * 128:(t + 1) * 128, :], f"x{s}", waits=w)

        def norm_T(t, which, dstT):
            s = t % HS
            col = state["st"] * 4
            state["st"] += 1
            t1 = P.op("act", lambda e: e.activation(out=JUNK[:], in_=H[:, s, :], func=AF.Square,
                                                    accum_out=ST[:, col:col + 1]),
                      waits=[h_ready[t], t_init])
            t2 = P.op("dve", lambda e: e.tensor_scalar(out=ST[:, col + 1:col + 2], in0=ST[:, col:col + 1],
                                                       scalar1=1.0 / D, scalar2=1e-6, op0=ALU.mult, op1=ALU.add),
                      waits=[t1])
            t3 = P.op("act", lambda e: e.activation(out=ST[:, col + 2:col + 3], in_=ST[:, col + 1:col + 2], func=AF.Ln),
                      waits=[t2])
            t4 = P.op("act", lambda e: e.activation(out=ST[:, col + 3:col + 4], in_=ST[:, col + 2:col + 3],
                                                    func=AF.Exp, scale=-0.5), waits=[t3])
            xs = state["xn"] % 2
            state["xn"] += 1
            t5 = P.op("act", lambda e: e.activation(out=XN[:, xs, :], in_=H[:, s, :], func=AF.Copy,
                                                    scale=ST[:, col + 3:col + 4]), waits=[t4, xn_free[xs]])
            t6 = None
            for k in range(8):
                t6 = P.op("pe", lambda e, k=k: e.transpose(out=PST[:, k, :], in_=XN[:, xs, k * 128:(k + 1) * 128],
                                                           identity=IDB[:]),
                          waits=[t5, c_tok] + bank_free["T"] if k == 0 else (), inc=(k == 7))
            xn_free[xs] = t6
            t7 = P.op("dve", lambda e: e.tensor_tensor(out=dstT, in0=PST[:], in1=bc(GFM[:, which, :], 2, 128),
                                                       op=ALU.mult), waits=[t6, c_tok])
            bank_free["T"] = [t7]
            return t7, col + 3

        gu_seq = [(n_, c_) for _p in range(NPASS) for n_ in (0, 1) for c_ in range(NCH)]
        gu_tok = {}
        gu_done = {}

        def plan_gu_load(g):
            if g >= len(gu_seq):
                return
            n_, c_ = gu_seq[g]
            slot = g % 4
            gu_tok[g] = P.dma("pool", WGU[:, slot, :], wgu_d[n_][c_], f"wgu{slot}", waits=[gu_done.get(g - 4)])

        for g0 in range(4):
            plan_gu_load(g0)

        def ffn(n, tiles, xnt_toks, on_last_tile=None):
            NTl = len(tiles)
            groups = [(o, min(512, NTl * 128 - o)) for o in range(0, NTl * 128, 512)]
            last_gate_up = None
            for b, chunks in enumerate(BLOCKS):
                nb = len(chunks)

                def plan_wd(chunks=chunks, nb=nb):
                    return P.dma("pool", WD[:, 0:nb, :],
                                 wd_d[n][chunks[0] * 128:(chunks[-1] + 1) * 128, :].rearrange("(c p) n -> p c n", p=128),
                                 "wd", waits=[wd_free[0]])
                tok_wd = None
                if b != 0:
                    tok_wd = plan_wd()
                if "wB" not in late and b >= 1:
                    late_pool(b)
                th = None
                for ci, c in enumerate(chunks):
                    g = state["gu"]
                    slot = g % 4
                    state["gu"] += 1
                    assert gu_seq[g] == (n, c)
                    tok_gu = gu_tok[g]
                    wv = WGU[:, slot, :].rearrange("p (g k j) -> p g k j", g=2, k=8)
                    tu = None
                    for (o, nn) in groups:
                        pr = state["pair"] % 2
                        state["pair"] += 1
                        bg, bu = 2 * pr, 2 * pr + 1
                        lastli = (o + nn) // 128 - 1
                        w0 = [tok_gu, xnt_toks[lastli]] + bank_free[bg] + bank_free[bu]
                        tg = None
                        for k in range(8):
                            tg = P.op("pe", lambda e, k=k, bg=bg, o=o, nn=nn, wv=wv: e.matmul(
                                out=PS[bg][:, 0:nn], lhsT=wv[:, 0, k, :], rhs=XNT[:, k, o:o + nn],
                                start=(k == 0), stop=(k == 7)), waits=w0 if k == 0 else (), inc=(k == 7))
                        for k in range(8):
                            tu = P.op("pe", lambda e, k=k, bu=bu, o=o, nn=nn, wv=wv: e.matmul(
                                out=PS[bu][:, 0:nn], lhsT=wv[:, 1, k, :], rhs=XNT[:, k, o:o + nn],
                                start=(k == 0), stop=(k == 7)), inc=(k == 7))
                        ss = state["sgs"] % 2
                        state["sgs"] += 1
                        ts = P.op("act", lambda e, bg=bg, nn=nn, ss=ss: e.activation(
                            out=SG[:, ss, 0:nn], in_=PS[bg][:, 0:nn], func=AF.Silu), waits=[tg, sg_free[ss]])
                        th = P.op("dve", lambda e, bu=bu, nn=nn, ss=ss, ci=ci, o=o: e.tensor_tensor(
                            out=HID[:, ci, o:o + nn], in0=PS[bu][:, 0:nn], in1=SG[:, ss, 0:nn], op=ALU.mult),
                            waits=[tu, ts])
                        bank_free[bg] = [ts]
                        bank_free[bu] = [th]
                        sg_free[ss] = th
                    gu_done[g] = tu
                    plan_gu_load(g + 4)
                    if b == 0 and ci == 1:
                        tok_wd = plan_wd()
                    last_gate_up = tu
                td = None
                four = not (on_last_tile is not None and b == len(BLOCKS) - 1 and n == 0)
                for li, t in enumerate(tiles):
                    s = t % HS
                    if four:
                        pairs = [(4, PS[4][:]), (5, PS[5][:])] if li % 2 == 0 else [(6, PS[6][:]), ("T", PSTF)]
                    else:
                        pairs = None
                    for nh in range(2):
                        if four:
                            bd, bap = pairs[nh]
                            w0 = ([tok_wd, th] + bank_free[pairs[0][0]] + bank_free[pairs[1][0]]) if nh == 0 else []
                        else:
                            bd = 4 + (state["dn"] % 2)
                            bap = PS[bd][:]
                            state["dn"] += 1
                            w0 = [tok_wd, th] + bank_free[bd]
                        for ci in range(nb):
                            td = P.op("pe", lambda e, ci=ci, bap=bap, li=li, nh=nh, nb=nb: e.matmul(
                                out=bap, lhsT=HID[:, ci, li * 128:(li + 1) * 128],
                                rhs=WD[:, ci, nh * 512:(nh + 1) * 512], start=(ci == 0), stop=(ci == nb - 1)),
                                waits=w0 if ci == 0 else (), inc=(ci == nb - 1))
                        te = P.op("dve", lambda e, bap=bap, s=s, nh=nh: e.scalar_tensor_tensor(
                            out=H[:, s, nh * 512:(nh + 1) * 512], in0=bap, scalar=0.5,
                            in1=H[:, s, nh * 512:(nh + 1) * 512], op0=ALU.mult, op1=ALU.add),
                            waits=[td, h_ready[t]])
                        bank_free[bd] = [te]
                        h_ready[t] = te
                    if on_last_tile is not None and b == len(BLOCKS) - 1:
                        on_last_tile(li)
                wd_free[0] = td
            return last_gate_up

        u_done = {}
        w_tok = {}

        def Na_sq(t, sq_eng="act"):
            s = t % HS
            col = state["st"] * 4
            state["st"] += 1
            if sq_eng == "act":
                t1 = P.op("act", lambda e: e.activation(out=JUNK[:], in_=H[:, s, :], func=AF.Square,
                                                        accum_out=ST[:, col:col + 1]),
                          waits=[h_ready[t], t_init])
            else:
                t1 = P.op("dve", lambda e: e.scalar_tensor_tensor(out=JUNK2[:], in0=H[:, s, :], scalar=1.0,
                                                                  in1=H[:, s, :], op0=ALU.mult, op1=ALU.mult,
                                                                  accum_out=ST[:, col:col + 1]),
                          waits=[h_ready[t], t_init, state.get("junk2")])
                state["junk2"] = t1
            return (col, t1)

        def Na_rest(t, ring, free_list, sq, copy_eng="act"):
            s = t % HS
            col, t1 = sq
            xs = ring[3] + state[ring[1]] % ring[2]
            state[ring[1]] += 1
            buf = ring[0]
            t3 = P.op("act", lambda e: e.activation(out=ST[:, col + 2:col + 3], in_=ST[:, col:col + 1], func=AF.Ln,
                                                    scale=1.0 / D, bias=EPS[:, 0:1]), waits=[t1])
            t4 = P.op("act", lambda e: e.activation(out=ST[:, col + 3:col + 4], in_=ST[:, col + 2:col + 3],
                                                    func=AF.Exp, scale=-0.5), waits=[t3])
            if copy_eng == "act":
                t5 = P.op("act", lambda e: e.activation(out=buf[:, xs, :], in_=H[:, s, :], func=AF.Copy,
                                                        scale=ST[:, col + 3:col + 4]), waits=[t4, free_list[xs]])
            else:
                t5 = P.op(copy_eng, lambda e: e.tensor_scalar(out=buf[:, xs, :], in0=H[:, s, :],
                                                            scalar1=ST[:, col + 3:col + 4], scalar2=None,
                                                            op0=ALU.mult), waits=[t4, free_list[xs]])
            return (t5, xs, buf, free_list)

        def Na(t, ring, free_list, sq_eng="act"):
            return Na_rest(t, ring, free_list, Na_sq(t, sq_eng))

        PS2B = PS[2][:].bitcast(BF16).rearrange("p (k t) -> p k t", k=8)

        def Nt(na, which, dstT, bank="T"):
            t5, xs, buf, free_list = na
            pt = PST if bank == "T" else PS2B
            t6 = None
            for k in range(8):
                t6 = P.op("pe", lambda e, k=k: e.transpose(out=pt[:, k, :], in_=buf[:, xs, k * 128:(k + 1) * 128],
                                                           identity=IDB[:]),
                          waits=[t5, c_tok] + bank_free[bank] if k == 0 else (), inc=(k == 7))
            free_list[xs] = t6
            t7 = P.op("dve", lambda e: e.tensor_tensor(out=dstT, in0=pt[:], in1=bc(GFM[:, which, :], 2, 128),
                                                       op=ALU.mult), waits=[t6, c_tok])
            bank_free[bank] = [t7]
            return t7

        def Wp(t, tn):
            us = t % 4
            bs = t % 2
            qs = t % 3
            segs = [(0, 512, 1), (512, 512, 2), (1024, 256, 3)]
            toks = []
            allw = []
            for (_co, _cn, _bk) in segs:
                allw += bank_free[_bk]
            for si_, (co, cn, bk) in enumerate(segs):
                w0 = ([tn, late["wB"]] + allw) if si_ == 0 else []
                tk = None
                for k in range(8):
                    tk = P.op("pe", lambda e, k=k, co=co, cn=cn, bk=bk: e.matmul(
                        out=PS[bk][:, 0:cn], lhsT=XNTB[:, bs, k, :], rhs=WIN[:, k, co:co + cn],
                        start=(k == 0), stop=(k == 7)), waits=w0 if k == 0 else (), inc=(k == 7))
                toks.append(tk)
            qv = QRAW[:, bs, :].rearrange("p (h d) -> p h d", d=64)
            kv_ = KRAW[:, bs, :].rearrange("p (h d) -> p h d", d=64)
            uq = U[:, us, 0:512].rearrange("p (h d) -> p h d", d=64)
            uk = U[:, us, 512:640].rearrange("p (h d) -> p h d", d=64)
            cosv = COS[:, t, :]
            sinv = SIN[:, t, :]
            tb0 = P.op("act", lambda e: e.activation(out=KRAW[:, bs, :], in_=PS[2][:, 0:128], func=AF.Copy),
                       waits=[toks[1]])
            ta0 = P.op("act", lambda e: e.activation(out=QRAW[:, bs, :], in_=PS[1][:, 0:512], func=AF.Copy),
                       waits=[toks[0], c_tok])
            ta = P.op("act", lambda e: e.activation(out=U[:, us, 0:512], in_=PS[1][:, 0:512], func=AF.Copy))
            tb = P.op("act", lambda e: e.activation(out=U[:, us, 512:640], in_=PS[2][:, 0:128], func=AF.Copy))
            last_rope = None
            for (src, dst, nh_, hoff) in ((qv, uq, 8, 0), (kv_, uk, 2, 8)):
                r1 = RT[:, bs, hoff:hoff + nh_, :]
                r2 = RT2[:, bs, hoff:hoff + nh_, :]
                a1 = P.op("dve", lambda e, src=src, r1=r1, nh_=nh_: e.tensor_tensor(
                    out=r1, in0=src[:, :, 0:16], in1=bc(cosv, 1, nh_), op=ALU.mult), waits=[ta0, tb0])
                a2 = P.op("dve", lambda e, src=src, r2=r2, nh_=nh_: e.tensor_tensor(
                    out=r2[:, :, 0:8], in0=src[:, :, 8:16], in1=bc(sinv[:, 0:8], 1, nh_), op=ALU.mult))
                a3 = P.op("dve", lambda e, src=src, r2=r2, nh_=nh_: e.tensor_tensor(
                    out=r2[:, :, 8:16], in0=src[:, :, 0:8], in1=bc(sinv[:, 8:16], 1, nh_), op=ALU.mult))
                a4 = P.op("dve", lambda e, dst=dst, r1=r1, r2=r2: e.tensor_tensor(
                    out=dst[:, :, 0:16], in0=r1, in1=r2, op=ALU.add), waits=[a1, a2, a3, ta, tb])
                last_rope = a4
            tv0 = P.op("act", lambda e: e.activation(out=VAUG[:, us, 0, 0:64], in_=PS[2][:, 128:192], func=AF.Copy),
                       waits=[toks[1], t_init])
            tv1 = P.op("act", lambda e: e.activation(out=VAUG[:, us, 1, 64:128], in_=PS[2][:, 192:256], func=AF.Copy))
            tp0 = P.op("act", lambda e: e.activation(out=U[:, us, 768:1024], in_=PS[2][:, 256:512], func=AF.Copy))
            tp1 = P.op("act", lambda e: e.activation(out=U[:, us, 1024:1280], in_=PS[3][:, 0:256], func=AF.Copy),
                       waits=[toks[2]])
            bank_free[1] = [ta]
            bank_free[2] = [tb, tp0]
            bank_free[3] = [tp1]
            w_tok[t] = (last_rope, [tv1, tp1, tp0])

        def QKT(t):
            us = t % 4
            qs = t % 3
            last_rope, others = w_tok[t]
            tt = None
            for c5 in range(5):
                tt = P.op("pe", lambda e, c5=c5: e.transpose(out=PST[:, c5, :], in_=U[:, us, c5 * 128:(c5 + 1) * 128],
                                                             identity=IDB[:]),
                          waits=[last_rope] + bank_free["T"] if c5 == 0 else (), inc=(c5 == 4))
            tq = P.op("act", lambda e: e.activation(out=QT[:, qs, :, :], in_=PST[:, 0:4, :], func=AF.Copy), waits=[tt])
            tk_ = P.op("act", lambda e: e.activation(out=KT[:, us, :], in_=PST[:, 4, :], func=AF.Copy))
            bank_free["T"] = [tq, tk_]
            u_done[t] = [tq, tk_, last_rope] + others

        mx = {}

        def Sc2(i):
            js = [j for j in (i - 1, i, i + 1) if 0 <= j < NT_ALL]
            need = []
            for j in js:
                need += u_done[j]
            ps_ = i % 2
            qs = i % 3
            sb_ = {0: [0, 1, 2], 1: [3, 4, 5]}
            tsc = {}
            first = True
            allb = []
            for bi, j in enumerate(js):
                for kv in range(2):
                    allb += bank_free[sb_[kv][bi]]
            for bi, j in enumerate(js):
                for kv in range(2):
                    r0, r1 = kv * 64, (kv + 1) * 64
                    bk = sb_[kv][bi]
                    diag = (j == i)
                    w0 = (need + [c_tok] + allb) if first else []
                    first = False
                    tsc[(kv, bi)] = P.op("pe", lambda e, bk=bk, j=j, r0=r0, r1=r1, diag=diag: e.matmul(
                        out=PS[bk][:], lhsT=KT[r0:r1, j % 4, :], rhs=QT[r0:r1, qs, :, :],
                        start=True, stop=diag), waits=w0, inc=diag)
            for bi, j in enumerate(js):
                if j == i:
                    continue
                which = 0 if j < i else 1
                for kv in range(2):
                    bk = sb_[kv][bi]
                    tsc[(kv, bi)] = P.op("pe", lambda e, bk=bk, which=which: e.matmul(
                        out=PS[bk][:], lhsT=IDB[:], rhs=NEGM[:, which, :], start=False, stop=True), inc=True)
            for kv in range(2):
                texp = []
                for bi, j in enumerate(js):
                    bk = sb_[kv][bi]
                    te = P.op("act", lambda e, bk=bk, bi=bi, kv=kv: e.activation(
                        out=PT[:, ps_, kv, bi, :], in_=PS[bk][:], func=AF.Exp, scale=0.125), waits=[tsc[(kv, bi)]])
                    bank_free[bk] = [te]
                    texp.append(te)
                mx[(i, kv)] = texp

        def Vp(i, kv):
            js = [j for j in (i - 1, i, i + 1) if 0 <= j < NT_ALL]
            texp = mx[(i, kv)]
            pb = 0 if kv == 0 else 4
            ps_ = i % 2
            bs = i % 2
            tpv = None
            for bi, j in enumerate(js):
                tpv = P.op("pe", lambda e, j=j, bi=bi: e.matmul(
                    out=PS[pb][:], lhsT=VAUG[:, j % 4, kv, :], rhs=PT[:, ps_, kv, bi, :],
                    start=(bi == 0), stop=False),
                    waits=texp + bank_free[pb] + [t_es, c_tok] if bi == 0 else (), inc=False)
            tpv = P.op("pe", lambda e: e.matmul(out=PS[pb][:], lhsT=ONEROW[0:1, kv, :], rhs=ESROW[0:1, kv, :],
                                                start=False, stop=True), inc=True)
            d0, d1 = (0, 64) if kv == 0 else (64, 128)
            e0, e1 = (64, 128) if kv == 0 else (0, 64)
            rv = RDEN[d0:d1, :].rearrange("p (h q) -> p h q", h=4)
            n1 = P.op("dve", lambda e: e.tensor_copy(out=RDEN[d0:d1, :], in_=PS[pb][e0:e1, :]), waits=[tpv])
            n2 = P.op("act", lambda e: e.activation(out=RDEN[d0:d1, :], in_=RDEN[d0:d1, :], func=AF.Ln), waits=[n1])
            n4 = P.op("act", lambda e: e.activation(out=RDEN[d0:d1, :], in_=RDEN[d0:d1, :], func=AF.Exp, scale=-1.0),
                      waits=[n2])
            mx[(i, "v", kv)] = (n1, n4)

        def Vn(i, kv):
            pb = 0 if kv == 0 else 4
            bs = i % 2
            d0, d1 = (0, 64) if kv == 0 else (64, 128)
            rv = RDEN[d0:d1, :].rearrange("p (h q) -> p h q", h=4)
            n1, n4 = mx[(i, "v", kv)]
            n5 = P.op("dve", lambda e: e.tensor_tensor(out=MIXT[d0:d1, bs, 0:4, :],
                                                       in0=PS[pb][d0:d1, :].rearrange("p (h q) -> p h q", h=4),
                                                       in1=rv, op=ALU.mult), waits=[n4])
            bank_free[pb] = [n1, n5]
            mx[(i, "n", kv)] = n5

        def Pd(i):
            js = [j for j in (i - 1, i, i + 1) if 0 <= j < NT_ALL]
            need = []
            for j in js:
                need += u_done[j]
            tpd = None
            first = True
            for g in range(4):
                for bi, j in enumerate(js):
                    if j < i:
                        v = 0
                    elif j > i:
                        v = 2
                    else:
                        v = 3 if i == 0 else (4 if i == NT_ALL - 1 else 1)
                    tpd = P.op("pe", lambda e, g=g, j=j, v=v, bi=bi: e.matmul(
                        out=PS[6][:, g * 128:(g + 1) * 128], lhsT=U[:, j % 4, 768 + g * 128:768 + (g + 1) * 128],
                        rhs=BPOOL[:, g * 5 + v, :], start=(bi == 0), stop=(bi == len(js) - 1)),
                        waits=need + bank_free[6] + [c_tok] if first else (),
                        inc=(g == 3 and bi == len(js) - 1))
                    first = False
            td_ = P.op("act", lambda e: e.activation(out=DT[:].rearrange("p g t -> p (g t)"), in_=PS[6][:], func=AF.Copy),
                       waits=[tpd])
            bank_free[6] = [td_]
            mx[(i, "d")] = td_

        def Py(i):
            bs = i % 2
            td_ = mx[(i, "d")]
            tpy = None
            for g in range(4):
                tpy = P.op("pe", lambda e, g=g: e.matmul(
                    out=PS[6][:, g * 128:(g + 1) * 128], lhsT=WPOOL[:, g, :], rhs=DT[:, g, :], start=True, stop=True),
                    waits=[td_, late["wB"]] + bank_free[6] if g == 0 else (), inc=(g == 3))
            ty = P.op("dve", lambda e: e.tensor_tensor(out=MIXT[:, bs, 4:8, :],
                                                       in0=PS[6][:].rearrange("p (g t) -> p g t", g=4),
                                                       in1=bc(PSCALE[:], 2, 128), op=ALU.mult), waits=[tpy, c_tok])
            bank_free[6] = [ty]
            mx[(i, "y")] = ty

        def Op(i):
            s = i % HS
            bs = i % 2
            for nh, bk in ((0, 5), (1, 6)):
                w0 = ([mx[(i, "n", 0)], mx[(i, "n", 1)], mx[(i, "y")], late["wB"]] + bank_free[5] + bank_free[6]) if nh == 0 else []
                to = None
                for c in range(8):
                    to = P.op("pe", lambda e, c=c, bk=bk, nh=nh: e.matmul(
                        out=PS[bk][:], lhsT=MIXT[:, bs, c, :], rhs=WOUT[:, c, nh * 512:(nh + 1) * 512],
                        start=(c == 0), stop=(c == 7)), waits=w0 if c == 0 else (), inc=(c == 7))
                te = P.op("dve", lambda e, bk=bk, nh=nh: e.tensor_tensor(
                    out=H[:, s, nh * 512:(nh + 1) * 512], in0=PS[bk][:], in1=H[:, s, nh * 512:(nh + 1) * 512],
                    op=ALU.add), waits=[to, h_ready[i]])
                bank_free[bk] = [te]
                h_ready[i] = te

        RING_N = (XN, "xn", 2, 0)
        RING_C = (XN, "xc", 2, 2)
        RING_A = (XN, "xa", 4, 0)
        xc_free = xn_free

        def stage_b_pro(b_tiles, w_list, na, done):
            pro = [t for t in w_list if t <= b_tiles[0] + 1]
            for t in pro:
                if t in done or t not in h_ready:
                    continue
                done.add(t)
                na[t] = Na(t, RING_N, xn_free)
                tn = Nt(na[t], 1, XNTB[:, t % 2, :, :])
                Wp(t, tn)
                QKT(t)

        def stage_b(p, b_tiles, w_list, na, done):
            xt_c = {}
            ca = {}
            wl = list(w_list)
            stage_b_pro(b_tiles, wl, na, done)
            rest = [t for t in wl if t > b_tiles[0] + 1]
            if rest:
                na[rest[0]] = Na(rest[0], RING_N, xn_free)
            for k in b_tiles:
                sqn = sqc = None
                if (k + 3) in rest:
                    sqn = Na_sq(k + 3, "dve")
                if k - 1 >= b_tiles[0]:
                    sqc = Na_sq(k - 1, "dve")
                Sc2(k)
                Pd(k)
                tn = None
                if (k + 2) in rest:
                    tn = Nt(na[k + 2], 1, XNTB[:, (k + 2) % 2, :, :])
                Vp(k, 0)
                Vp(k, 1)
                Py(k)
                Vn(k, 0)
                Vn(k, 1)
                if sqn is not None:
                    na[k + 3] = Na_rest(k + 3, RING_N, xn_free, sqn, copy_eng="dve")
                if sqc is not None:
                    ca[k - 1] = Na_rest(k - 1, RING_C, xc_free, sqc, copy_eng="dve")
                if (k + 2) in rest:
                    Wp(k + 2, tn)
                Op(k)
                if (k + 2) in rest:
                    QKT(k + 2)
                if k - 1 >= b_tiles[0]:
                    li = k - 1 - b_tiles[0]
                    xt_c[k - 1] = Nt(ca[k - 1], 2, XNT[:, :, li * 128:(li + 1) * 128], bank=2)
            kl = b_tiles[-1]
            ca[kl] = Na(kl, RING_C, xc_free)
            li = kl - b_tiles[0]
            xt_c[kl] = Nt(ca[kl], 2, XNT[:, :, li * 128:(li + 1) * 128])
            return [xt_c[t] for t in b_tiles]

        def final(t):
            s = t % HS
            col = state["st"] * 4
            state["st"] += 1
            t1 = P.op("act", lambda e: e.activation(out=JUNK[:], in_=H[:, s, :], func=AF.Square,
                                                    accum_out=ST[:, col:col + 1]), waits=[h_ready[t], t_init])
            t3 = P.op("act", lambda e: e.activation(out=ST[:, col + 2:col + 3], in_=ST[:, col:col + 1], func=AF.Ln,
                                                    scale=1.0 / D, bias=EPS[:, 0:1]), waits=[t1])
            t4 = P.op("act", lambda e: e.activation(out=ST[:, col + 3:col + 4], in_=ST[:, col + 2:col + 3],
                                                    func=AF.Exp, scale=-0.5), waits=[t3])
            t5 = P.op("dve", lambda e: e.scalar_tensor_tensor(out=H[:, s, :], in0=H[:, s, :],
                                                              scalar=ST[:, col + 3:col + 4], in1=GFIN[:],
                                                              op0=ALU.mult, op1=ALU.mult), waits=[t4, c_tok])
            store_tok[t] = P.dma("sp", out_d[t * 128:(t + 1) * 128, :], H[:, s, :], f"o{s}", waits=[t5])

        a_lists = []
        nxt = 0
        for p in range(NPASS):
            a_hi = min(NT_ALL, (p + 1) * TPP + 1)
            a_lists.append(list(range(nxt, a_hi)))
            nxt = a_hi
        loaded = set()

        def ensure_load(t):
            if t not in loaded and t < NT_ALL:
                loaded.add(t)
                load_x(t)

        xnt_free = None
        na_prev = {}
        pending_fin = []
        for p in range(NPASS):
            b_tiles = list(range(p * TPP, (p + 1) * TPP))
            a_tiles = a_lists[p]
            next_a = a_lists[p + 1] if p + 1 < NPASS else []
            for t in a_tiles:
                ensure_load(t)
            na = dict(na_prev)
            for t in a_tiles[:4]:
                if t not in na:
                    na[t] = Na(t, RING_A, xn_free, sq_eng=("dve" if t % 2 == 0 else "act"))
            xt = []
            if xnt_free is not None:
                P.wait_only("dve", [xnt_free])
            for li, t in enumerate(a_tiles):
                xt.append(Nt(na[t], 0, XNT[:, :, li * 128:(li + 1) * 128], bank=("T" if li % 2 == 0 else 2)))
                if li + 4 < len(a_tiles):
                    t4_ = a_tiles[li + 4]
                    na[t4_] = Na(t4_, RING_A, xn_free, sq_eng=("dve" if t4_ % 2 == 0 else "act"))
                if li == 3 and pending_fin:
                    pending_fin.pop()()
            if pending_fin:
                pending_fin.pop()()
            if p == 0:
                late_sp()
            nb_ = {}
            done_ = set()
            npro = len([t for t in a_tiles if t <= b_tiles[0] + 1])

            sched = {}
            pro_tiles = [t for t in a_tiles if t <= b_tiles[0] + 1]
            tn_ = {}
            for j, t in enumerate(pro_tiles):
                base = len(pro_tiles) + j

                def s_na(t=t, nb_=nb_, done_=done_):
                    done_.add(t)
                    nb_[t] = Na(t, RING_N, xn_free)

                def s_nt(t=t, nb_=nb_, tn_=tn_):
                    tn_[t] = Nt(nb_[t], 1, XNTB[:, t % 2, :, :])

                def s_w(t=t, tn_=tn_):
                    Wp(t, tn_[t])

                def s_q(t=t):
                    QKT(t)
                sched.setdefault(base, []).append(s_na)
                sched.setdefault(base + 1, []).append(s_nt)
                sched.setdefault(base + 2, []).append(s_w)
                sched.setdefault(base + 4, []).append(s_q)
            last_li = len(a_tiles) - 1
            pending = []

            def cb1(li, sched=sched, last_li=last_li):
                for k_ in sorted(sched):
                    if k_ <= li or li == last_li:
                        for f_ in sched.pop(k_):
                            f_()

            xnt_free = ffn(0, a_tiles, xt, on_last_tile=cb1)
            P.wait_only("dve", [xnt_free])
            xt = stage_b(p, b_tiles, a_tiles, nb_, done_)
            if next_a:
                ensure_load(next_a[0])

            def fin(t):
                final(t)
                if (t + HS) in next_a:
                    ensure_load(t + HS)

            na_next = {}

            def cb(li, b_tiles=b_tiles, next_a=next_a, na_next=na_next):
                if li >= 1:
                    fin(b_tiles[li - 1])
                if li >= 4 and li - 4 < len(next_a):
                    tn_ = next_a[li - 4]
                    ensure_load(tn_)
                    na_next[tn_] = Na(tn_, RING_A, xn_free, sq_eng=("dve" if tn_ % 2 == 0 else "act"))

            xnt_free = ffn(1, b_tiles, xt, on_last_tile=cb)
            if p + 1 < NPASS:
                pending_fin.append(lambda fin=fin, t=b_tiles[-1]: fin(t))
            else:
                fin(b_tiles[-1])
            na_prev = na_next
        P.wait_only("sp", [store_tok[t] for t in range(NT_ALL)])

        block = E(nc.Block())

        @block.tensor
        def _(e):
            P.emit("pe", e, sems)

        @block.scalar
        def _(e):
            P.emit("act", e, sems)

        @block.vector
        def _(e):
            P.emit("dve", e, sems)

        @block.gpsimd
        def _(e):
            P.emit("pool", e, sems)

        @block.sync
        def _(e):
            P.emit("sp", e, sems)
    return nc


def _pool_consts(S=S):
    NT_ALL = S // 128
    wins = (2, 4, 8, 16)
    B = np.zeros((128, 20, 128), np.float64)
    for g, w in enumerate(wins):
        half = w // 2
        for (tile_i, variants) in ((0, {0: 3, 1: 2}), (1, {0: 0, 1: 1, 2: 2}), (NT_ALL - 1, {NT_ALL - 2: 0, NT_ALL - 1: 4})):
            for tl in range(128):
                t = tile_i * 128 + tl
                coef = {}
                for (lo, hi) in ((t - half, t + half - 1), (t - half + 1, t + half)):
                    a = min(max(lo, 0), S)
                    b = min(max(hi + 1, 0), S)
                    for tp in range(a, b):
                        coef[tp] = coef.get(tp, 0.0) + 0.5 / (b - a)
                coef[t] = coef.get(t, 0.0) - 1.0
                for tp, cval in coef.items():
                    j = tp // 128
                    if j not in variants:
                        continue
                    B[tp % 128, g * 5 + variants[j], tl] = cval
    return B.astype(np.float32)


def _consts(S=S):
    NT_ALL = S // 128
    ident = np.eye(128, dtype=np.float32)
    b = np.arange(128)[:, None]
    a = np.arange(128)[None, :]
    m0 = np.where(a > b, NEG, 0.0).astype(np.float32)
    m1 = np.where(b > a, NEG, 0.0).astype(np.float32)
    negm = np.stack([np.tile(m0, (1, 4)), np.tile(m1, (1, 4))], axis=1)
    inv_freq = np.float32(500000.0) ** (-(np.arange(0, 16, 2, dtype=np.float32) / np.float32(16)))
    ang = np.arange(S, dtype=np.float32)[:, None] * inv_freq[None, :].astype(np.float32)
    emb = np.concatenate([ang, ang], axis=-1).astype(np.float32)
    cos = np.cos(emb.astype(np.float64)).astype(np.float32)
    sin = np.sin(emb.astype(np.float64)).astype(np.float32)
    sin_s = sin.copy()
    sin_s[:, 0:8] = -sin[:, 0:8]
    cosT = np.ascontiguousarray(cos.reshape(NT_ALL, 128, 16).transpose(1, 0, 2))
    sinT = np.ascontiguousarray(sin_s.reshape(NT_ALL, 128, 16).transpose(1, 0, 2))
    return ident, np.ascontiguousarray(negm), _pool_consts(S), cosT, sinT


_CACHE = {}


def prep_shared(ffn1_norm, ffn1_w_gate, ffn1_w_up, ffn1_w_down, mix_norm, w_in,
                sink_logits, pool_w, pool_scale, w_out, ffn2_norm, ffn2_w_gate,
                ffn2_w_up, ffn2_w_down, final_norm, S_=S):
    f = lambda a: np.ascontiguousarray(np.asarray(a, dtype=np.float32))
    ident, negm, bpool, cosT, sinT = _consts(S_)

    def gu_layout(wg, wu):
        arr = np.stack([f(wg)[0], f(wu)[0]])
        arr = arr.reshape(2, 8, 128, NCH, 128)
        arr = arr.transpose(3, 2, 0, 1, 4)
        return np.ascontiguousarray(arr).reshape(NCH, 128, 2 * 8 * 128)

    perm_q = []
    for c in range(4):
        perm_q += list(range(c * 64, (c + 1) * 64)) + list(range((4 + c) * 64, (5 + c) * 64))
    perm_q = np.array(perm_q)
    col_perm = np.concatenate([perm_q, np.arange(512, 1280)])
    row_perm = np.concatenate([perm_q, np.arange(512, 1024)])
    win_p = np.ascontiguousarray(f(w_in)[0][:, col_perm])
    wout_p = np.ascontiguousarray(f(w_out)[0][row_perm, :])
    wpool = np.ascontiguousarray(f(pool_w)[0].transpose(1, 0, 2))
    gs = np.stack([f(ffn1_norm)[0], f(mix_norm)[0], f(ffn2_norm)[0]])
    gfm = np.ascontiguousarray(gs.reshape(3, 8, 128).transpose(2, 0, 1))
    gfin = np.ascontiguousarray(np.broadcast_to(f(final_norm)[None, :], (128, D)))
    pscale = np.ascontiguousarray(f(pool_scale)[0].reshape(4, 128).T)
    sk = f(sink_logits)[0]
    sinkb = np.ascontiguousarray(np.concatenate([np.broadcast_to(sk[0:4][None, :], (64, 4)),
                                                 np.broadcast_to(sk[4:8][None, :], (64, 4))], axis=0))
    return {
        "wgu1": gu_layout(ffn1_w_gate, ffn1_w_up), "wgu2": gu_layout(ffn2_w_gate, ffn2_w_up),
        "wd1": f(ffn1_w_down)[0], "wd2": f(ffn2_w_down)[0],
        "win": win_p, "wout": wout_p, "wpool": wpool, "ident": ident, "negm": negm, "bpool": bpool,
        "cosT": cosT, "sinT": sinT, "gfm": gfm, "gfin": gfin, "pscale": pscale, "sinkb": sinkb,
    }


def kernel(x, **params):
    x = np.ascontiguousarray(np.asarray(x, dtype=np.float32))
    if "nc" not in _CACHE:
        _CACHE["nc"] = build_program()
    nc = _CACHE["nc"]
    shared = prep_shared(**params)
    in_maps = []
    for c in range(8):
        m = dict(shared)
        m["x"] = x[c]
        in_maps.append(m)
    res = run_bass_kernel_spmd(nc, in_maps, core_ids=list(range(8)))
    return np.stack([np.asarray(r["out"], dtype=np.float32) for r in res.results], axis=0)
```

```python
import numpy as np
from contextlib import ExitStack

import concourse.bass as bass
import concourse.mybir as mybir
from concourse.bass_utils import run_bass_kernel_spmd

F32 = mybir.dt.float32
BF16 = mybir.dt.bfloat16
AF = mybir.ActivationFunctionType
ALU = mybir.AluOpType

S = 4096
D = 1024
DFF = 2816
NT_ALL = 32
NCH = 22
BLOCKS = [list(range(0, 6)), list(range(6, 12)), list(range(12, 17)), list(range(17, 22))]
NPASS = 4
TPP = 8
HS = 10
NEG = -30000.0


def bc(ap, axis, n):
    shp = list(ap.shape)
    a = ap.unsqueeze(axis)
    shp.insert(axis, n)
    return a.broadcast_to(shp)


class Plan:
    ENGS = ("pe", "act", "dve", "pool", "sp")

    def __init__(self):
        self.ops = {e: [] for e in self.ENGS}
        self.cnt = {}
        self.seen = {e: {} for e in self.ENGS}

    def _filter(self, eng, waits):
        w = []
        for t in waits:
            if t is None:
                continue
            if isinstance(t, list):
                for tt in t:
                    waits.append(tt)
                continue
            key, val = t
            if self.seen[eng].get(key, 0) >= val:
                continue
            self.seen[eng][key] = val
            w.append((key, val))
        return w

    def op(self, eng, fn, waits=(), inc=True):
        w = self._filter(eng, list(waits))
        tok = None
        incinfo = None
        if inc:
            self.cnt[eng] = self.cnt.get(eng, 0) + 1
            tok = (eng, self.cnt[eng])
            incinfo = (eng, 1)
        self.ops[eng].append((w, fn, incinfo))
        return tok

    def dma(self, eng, out, in_, key, waits=()):
        w = self._filter(eng, list(waits))
        self.cnt[key] = self.cnt.get(key, 0) + 16
        tok = (key, self.cnt[key])
        self.ops[eng].append((w, lambda e, o=out, i=in_: e.dma_start(out=o, in_=i), (key, 16)))
        return tok

    def wait_only(self, eng, waits):
        w = self._filter(eng, list(waits))
        if w:
            self.ops[eng].append((w, None, None))

    def emit(self, eng, e, sems):
        for (w, fn, incinfo) in self.ops[eng]:
            for (key, val) in w:
                e.wait_ge(sems[key], val)
            if fn is None:
                continue
            ins = fn(e)
            if incinfo is not None:
                ins.then_inc(sems[incinfo[0]], incinfo[1])


def build_program(S=S):
    NT_ALL = S // 128
    NPASS = NT_ALL // TPP
    nc = bass.Bass("TRN2", target_bir_lowering=False)
    dt = nc.dram_tensor
    x_d = dt("x", [S, D], F32, kind="ExternalInput").ap()
    wgu_d = [dt("wgu1", [NCH, 128, 2 * 8 * 128], F32, kind="ExternalInput").ap(),
             dt("wgu2", [NCH, 128, 2 * 8 * 128], F32, kind="ExternalInput").ap()]
    wd_d = [dt("wd1", [DFF, D], F32, kind="ExternalInput").ap(),
            dt("wd2", [DFF, D], F32, kind="ExternalInput").ap()]
    win_d = dt("win", [D, 1280], F32, kind="ExternalInput").ap()
    wout_d = dt("wout", [D, D], F32, kind="ExternalInput").ap()
    wpool_d = dt("wpool", [128, 4, 128], F32, kind="ExternalInput").ap()
    ident_d = dt("ident", [128, 128], F32, kind="ExternalInput").ap()
    negm_d = dt("negm", [128, 2, 512], F32, kind="ExternalInput").ap()
    bpool_d = dt("bpool", [128, 20, 128], F32, kind="ExternalInput").ap()
    cos_d = dt("cosT", [128, NT_ALL, 16], F32, kind="ExternalInput").ap()
    sin_d = dt("sinT", [128, NT_ALL, 16], F32, kind="ExternalInput").ap()
    gfm_d = dt("gfm", [128, 3, 8], F32, kind="ExternalInput").ap()
    gfin_d = dt("gfin", [128, D], F32, kind="ExternalInput").ap()
    pscale_d = dt("pscale", [128, 4], F32, kind="ExternalInput").ap()
    sink_d = dt("sinkb", [128, 4], F32, kind="ExternalInput").ap()
    out_d = dt("out", [S, D], F32, kind="ExternalOutput").ap()

    with ExitStack() as es:
        E = es.enter_context
        sb = lambda name, shape, dty: E(nc.sbuf_tensor(name, shape, dty))
        H = sb("H", [128, HS, D], F32)
        XNT = sb("XNT", [128, 8, 9 * 128], BF16)
        HID = sb("HID", [128, 6, 9 * 128], BF16)
        WD = sb("WD", [128, 6, D], BF16)
        WGU = sb("WGU", [128, 4, 2 * 8 * 128], BF16)
        SG = sb("SG", [128, 2, 512], BF16)
        JUNK = sb("JUNK", [128, D], BF16)
        JUNK2 = sb("JUNK2", [128, D], BF16)
        XN = sb("XN", [128, 4, D], BF16)
        EPS = sb("EPS", [128, 1], F32)
        ST = sb("ST", [128, 4 * 4 * NT_ALL + 64], F32)
        WIN = sb("WIN", [128, 8, 1280], BF16)
        WOUT = sb("WOUT", [128, 8, D], BF16)
        WPOOL = sb("WPOOL", [128, 4, 128], BF16)
        IDB = sb("IDB", [128, 128], BF16)
        NEGM = sb("NEGM", [128, 2, 512], BF16)
        BPOOL = sb("BPOOL", [128, 20, 128], BF16)
        COS = sb("COS", [128, NT_ALL, 16], F32)
        SIN = sb("SIN", [128, NT_ALL, 16], F32)
        GFM = sb("GFM", [128, 3, 8], F32)
        GFIN = sb("GFIN", [128, D], F32)
        PSCALE = sb("PSCALE", [128, 4], F32)
        SINKB = sb("SINKB", [128, 4], F32)
        ESINK = sb("ESINK", [128, 4], F32)
        ESROW = sb("ESROW", [1, 2, 512], BF16)
        ONEROW = sb("ONEROW", [1, 2, 128], BF16)
        U = sb("U", [128, 4, 1280], BF16)
        VAUG = sb("VAUG", [128, 4, 2, 128], BF16)
        QT = sb("QT", [128, 3, 4, 128], BF16)
        KT = sb("KT", [128, 4, 128], BF16)
        XNTB = sb("XNTB", [128, 2, 8, 128], BF16)
        MIXT = sb("MIXT", [128, 2, 8, 128], BF16)
        RDEN = sb("RDEN", [128, 512], F32)
        DT = sb("DT", [128, 4, 128], BF16)
        KRAW = sb("KRAW", [128, 2, 128], F32)
        QRAW = sb("QRAW", [128, 2, 512], F32)
        RT = sb("RT", [128, 2, 10, 16], F32)
        RT2 = sb("RT2", [128, 2, 10, 16], F32)
        PS = [E(nc.psum_tensor(f"ps{i}", [128, 512], F32)) for i in range(7)]
        PST = E(nc.psum_tensor("pst", [128, 8, 128], BF16))
        PSTF = PST[:].rearrange("p a b -> p (a b)").bitcast(F32)
        PT = HID[:].rearrange("p a b -> p (a b)")[:, 0:6144].rearrange("p (s k b n) -> p s k b n", s=2, k=2, b=3)

        P = Plan()
        sem_keys = ["pe", "act", "dve", "pool", "c", "cg", "c2", "cg2", "wd", "wB"] + [f"x{i}" for i in range(HS)] + \
                   [f"o{i}" for i in range(HS)] + [f"wgu{i}" for i in range(4)]
        sems = {k: E(nc.semaphore(k)) for k in sem_keys}

        bank_free = {i: [] for i in range(7)}
        bank_free["T"] = []
        h_ready = {}
        store_tok = {}
        state = {"gu": 0, "pair": 0, "sgs": 0, "dn": 0, "xn": 0, "xc": 0, "xa": 0, "st": 0}
        gu_free = [None] * 4
        sg_free = [None] * 2
        xn_free = [None] * 4
        wd_free = [None]

        P.dma("sp", GFM[:], gfm_d, "c")
        P.dma("sp", PSCALE[:], pscale_d, "c")
        c_sp = P.dma("sp", SINKB[:], sink_d, "c")
        tok_idb = P.dma("pool", IDB[:], ident_d, "cg")
        c_tok = [c_sp, tok_idb]
        late = {}

        def late_sp():
            P.dma("sp", COS[:], cos_d, "c2")
            P.dma("sp", SIN[:], sin_d, "c2")
            c_tok.append(P.dma("sp", GFIN[:], gfin_d, "c2"))

        def late_pool(b):
            if b == 1:
                P.dma("pool", WIN[:], win_d.rearrange("(k p) n -> p k n", p=128), "wB")
                late["b1"] = True
            elif b == 2:
                P.dma("pool", WOUT[:], wout_d.rearrange("(k p) n -> p k n", p=128), "wB")
                late["b2"] = True
            elif b == 3:
                P.dma("pool", NEGM[:], negm_d, "cg2")
                c_tok.append(P.dma("pool", BPOOL[:], bpool_d, "cg2"))
                late["wB"] = P.dma("pool", WPOOL[:], wpool_d, "wB")

        t_init = P.op("dve", lambda e: e.memset(ST[:], 0.0))
        t_init = P.op("dve", lambda e: e.memset(EPS[:], 1e-6))
        t_init = P.op("dve", lambda e: e.memset(VAUG[:], 1.0))
        t_es = P.op("act", lambda e: e.activation(out=ESINK[:], in_=SINKB[:], func=AF.Exp), waits=[c_tok])
        t_es1 = P.op("act", lambda e: e.activation(out=ESROW[0:1, 0, :].rearrange("p (h q) -> p h q", h=4),
                                                   in_=bc(ESINK[0:1, :], 2, 128), func=AF.Copy), waits=[t_es])
        t_es = P.op("act", lambda e: e.activation(out=ESROW[0:1, 1, :].rearrange("p (h q) -> p h q", h=4),
                                                  in_=bc(ESINK[64:65, :], 2, 128), func=AF.Copy), waits=[t_es])
        P.op("dve", lambda e: e.memset(ONEROW[0:1, 0, 0:64], 0.0))
        P.op("dve", lambda e: e.memset(ONEROW[0:1, 0, 64:128], 1.0))
        P.op("dve", lambda e: e.memset(ONEROW[0:1, 1, 0:64], 1.0))
        t_one = P.op("dve", lambda e: e.memset(ONEROW[0:1, 1, 64:128], 0.0))
        t_es = [t_es, t_one]

        def load_x(t):
            s = t % HS
            w = [store_tok.get(t - HS)]
            h_ready[t] = P.dma("sp", H[:, s, :], x_d[t * 128:(t + 1) * 128, :], f"x{s}", waits=w)

        def norm_T(t, which, dstT):
            s = t % HS
            col = state["st"] * 4
            state["st"] += 1
            t1 = P.op("act", lambda e: e.activation(out=JUNK[:], in_=H[:, s, :], func=AF.Square,
                                                    accum_out=ST[:, col:col + 1]),
                      waits=[h_ready[t], t_init])
            t2 = P.op("dve", lambda e: e.tensor_scalar(out=ST[:, col + 1:col + 2], in0=ST[:, col:col + 1],
                                                       scalar1=1.0 / D, scalar2=1e-6, op0=ALU.mult, op1=ALU.add),
                      waits=[t1])
            t3 = P.op("act", lambda e: e.activation(out=ST[:, col + 2:col + 3], in_=ST[:, col + 1:col + 2], func=AF.Ln),
                      waits=[t2])
            t4 = P.op("act", lambda e: e.activation(out=ST[:, col + 3:col + 4], in_=ST[:, col + 2:col + 3],
                                                    func=AF.Exp, scale=-0.5), waits=[t3])
            xs = state["xn"] % 2
            state["xn"] += 1
            t5 = P.op("act", lambda e: e.activation(out=XN[:, xs, :], in_=H[:, s, :], func=AF.Copy,
                                                    scale=ST[:, col + 3:col + 4]), waits=[t4, xn_free[xs]])
            t6 = None
            for k in range(8):
                t6 = P.op("pe", lambda e, k=k: e.transpose(out=PST[:, k, :], in_=XN[:, xs, k * 128:(k + 1) * 128],
                                                           identity=IDB[:]),
                          waits=[t5, c_tok] + bank_free["T"] if k == 0 else (), inc=(k == 7))
            xn_free[xs] = t6
            t7 = P.op("dve", lambda e: e.tensor_tensor(out=dstT, in0=PST[:], in1=bc(GFM[:, which, :], 2, 128),
                                                       op=ALU.mult), waits=[t6, c_tok])
            bank_free["T"] = [t7]
            return t7, col + 3

        gu_seq = [(n_, c_) for _p in range(NPASS) for n_ in (0, 1) for c_ in range(NCH)]
        gu_tok = {}
        gu_done = {}

        def plan_gu_load(g):
            if g >= len(gu_seq):
                return
            n_, c_ = gu_seq[g]
            slot = g % 4
            gu_tok[g] = P.dma("pool", WGU[:, slot, :], wgu_d[n_][c_], f"wgu{slot}", waits=[gu_done.get(g - 4)])

        for g0 in range(4):
            plan_gu_load(g0)

        def ffn(n, tiles, xnt_toks, on_last_tile=None):
            NTl = len(tiles)
            groups = [(o, min(512, NTl * 128 - o)) for o in range(0, NTl * 128, 512)]
            last_gate_up = None
            for b, chunks in enumerate(BLOCKS):
                nb = len(chunks)

                def plan_wd(chunks=chunks, nb=nb):
                    return P.dma("pool", WD[:, 0:nb, :],
                                 wd_d[n][chunks[0] * 128:(chunks[-1] + 1) * 128, :].rearrange("(c p) n -> p c n", p=128),
                                 "wd", waits=[wd_free[0]])
                tok_wd = None
                if b != 0:
                    tok_wd = plan_wd()
                if "wB" not in late and b >= 1:
                    late_pool(b)
                th = None
                for ci, c in enumerate(chunks):
                    g = state["gu"]
                    slot = g % 4
                    state["gu"] += 1
                    assert gu_seq[g] == (n, c)
                    tok_gu = gu_tok[g]
                    wv = WGU[:, slot, :].rearrange("p (g k j) -> p g k j", g=2, k=8)
                    tu = None
                    for (o, nn) in groups:
                        pr = state["pair"] % 2
                        state["pair"] += 1
                        bg, bu = 2 * pr, 2 * pr + 1
                        lastli = (o + nn) // 128 - 1
                        w0 = [tok_gu, xnt_toks[lastli]] + bank_free[bg] + bank_free[bu]
                        tg = None
                        for k in range(8):
                            tg = P.op("pe", lambda e, k=k, bg=bg, o=o, nn=nn, wv=wv: e.matmul(
                                out=PS[bg][:, 0:nn], lhsT=wv[:, 0, k, :], rhs=XNT[:, k, o:o + nn],
                                start=(k == 0), stop=(k == 7)), waits=w0 if k == 0 else (), inc=(k == 7))
                        for k in range(8):
                            tu = P.op("pe", lambda e, k=k, bu=bu, o=o, nn=nn, wv=wv: e.matmul(
                                out=PS[bu][:, 0:nn], lhsT=wv[:, 1, k, :], rhs=XNT[:, k, o:o + nn],
                                start=(k == 0), stop=(k == 7)), inc=(k == 7))
                        ss = state["sgs"] % 2
                        state["sgs"] += 1
                        ts = P.op("act", lambda e, bg=bg, nn=nn, ss=ss: e.activation(
                            out=SG[:, ss, 0:nn], in_=PS[bg][:, 0:nn], func=AF.Silu), waits=[tg, sg_free[ss]])
                        th = P.op("dve", lambda e, bu=bu, nn=nn, ss=ss, ci=ci, o=o: e.tensor_tensor(
                            out=HID[:, ci, o:o + nn], in0=PS[bu][:, 0:nn], in1=SG[:, ss, 0:nn], op=ALU.mult),
                            waits=[tu, ts])
                        bank_free[bg] = [ts]
                        bank_free[bu] = [th]
                        sg_free[ss] = th
                    gu_done[g] = tu
                    plan_gu_load(g + 4)
                    if b == 0 and ci == 1:
                        tok_wd = plan_wd()
                    last_gate_up = tu
                td = None
                four = not (on_last_tile is not None and b == len(BLOCKS) - 1 and n == 0)
                for li, t in enumerate(tiles):
                    s = t % HS
                    if four:
                        pairs = [(4, PS[4][:]), (5, PS[5][:])] if li % 2 == 0 else [(6, PS[6][:]), ("T", PSTF)]
                    else:
                        pairs = None
                    for nh in range(2):
                        if four:
                            bd, bap = pairs[nh]
                            w0 = ([tok_wd, th] + bank_free[pairs[0][0]] + bank_free[pairs[1][0]]) if nh == 0 else []
                        else:
                            bd = 4 + (state["dn"] % 2)
                            bap = PS[bd][:]
                            state["dn"] += 1
                            w0 = [tok_wd, th] + bank_free[bd]
                        for ci in range(nb):
                            td = P.op("pe", lambda e, ci=ci, bap=bap, li=li, nh=nh, nb=nb: e.matmul(
                                out=bap, lhsT=HID[:, ci, li * 128:(li + 1) * 128],
                                rhs=WD[:, ci, nh * 512:(nh + 1) * 512], start=(ci == 0), stop=(ci == nb - 1)),
                                waits=w0 if ci == 0 else (), inc=(ci == nb - 1))
                        te = P.op("dve", lambda e, bap=bap, s=s, nh=nh: e.scalar_tensor_tensor(
                            out=H[:, s, nh * 512:(nh + 1) * 512], in0=bap, scalar=0.5,
                            in1=H[:, s, nh * 512:(nh + 1) * 512], op0=ALU.mult, op1=ALU.add),
                            waits=[td, h_ready[t]])
                        bank_free[bd] = [te]
                        h_ready[t] = te
                    if on_last_tile is not None and b == len(BLOCKS) - 1:
                        on_last_tile(li)
                wd_free[0] = td
            return last_gate_up

        u_done = {}
        w_tok = {}

        def Na_sq(t, sq_eng="act"):
            s = t % HS
            col = state["st"] * 4
            state["st"] += 1
            if sq_eng == "act":
                t1 = P.op("act", lambda e: e.activation(out=JUNK[:], in_=H[:, s, :], func=AF.Square,
                                                        accum_out=ST[:, col:col + 1]),
                          waits=[h_ready[t], t_init])
            else:
                t1 = P.op("dve", lambda e: e.scalar_tensor_tensor(out=JUNK2[:], in0=H[:, s, :], scalar=1.0,
                                                                  in1=H[:, s, :], op0=ALU.mult, op1=ALU.mult,
                                                                  accum_out=ST[:, col:col + 1]),
                          waits=[h_ready[t], t_init, state.get("junk2")])
                state["junk2"] = t1
            return (col, t1)

        def Na_rest(t, ring, free_list, sq, copy_eng="act"):
            s = t % HS
            col, t1 = sq
            xs = ring[3] + state[ring[1]] % ring[2]
            state[ring[1]] += 1
            buf = ring[0]
            t3 = P.op("act", lambda e: e.activation(out=ST[:, col + 2:col + 3], in_=ST[:, col:col + 1], func=AF.Ln,
                                                    scale=1.0 / D, bias=EPS[:, 0:1]), waits=[t1])
            t4 = P.op("act", lambda e: e.activation(out=ST[:, col + 3:col + 4], in_=ST[:, col + 2:col + 3],
                                                    func=AF.Exp, scale=-0.5), waits=[t3])
            if copy_eng == "act":
                t5 = P.op("act", lambda e: e.activation(out=buf[:, xs, :], in_=H[:, s, :], func=AF.Copy,
                                                        scale=ST[:, col + 3:col + 4]), waits=[t4, free_list[xs]])
            else:
                t5 = P.op(copy_eng, lambda e: e.tensor_scalar(out=buf[:, xs, :], in0=H[:, s, :],
                                                            scalar1=ST[:, col + 3:col + 4], scalar2=None,
                                                            op0=ALU.mult), waits=[t4, free_list[xs]])
            return (t5, xs, buf, free_list)

        def Na(t, ring, free_list, sq_eng="act"):
            return Na_rest(t, ring, free_list, Na_sq(t, sq_eng))

        PS2B = PS[2][:].bitcast(BF16).rearrange("p (k t) -> p k t", k=8)

        def Nt(na, which, dstT, bank="T"):
            t5, xs, buf, free_list = na
            pt = PST if bank == "T" else PS2B
            t6 = None
            for k in range(8):
                t6 = P.op("pe", lambda e, k=k: e.transpose(out=pt[:, k, :], in_=buf[:, xs, k * 128:(k + 1) * 128],
                                                           identity=IDB[:]),
                          waits=[t5, c_tok] + bank_free[bank] if k == 0 else (), inc=(k == 7))
            free_list[xs] = t6
            t7 = P.op("dve", lambda e: e.tensor_tensor(out=dstT, in0=pt[:], in1=bc(GFM[:, which, :], 2, 128),
                                                       op=ALU.mult), waits=[t6, c_tok])
            bank_free[bank] = [t7]
            return t7

        def Wp(t, tn):
            us = t % 4
            bs = t % 2
            qs = t % 3
            segs = [(0, 512, 1), (512, 512, 2), (1024, 256, 3)]
            toks = []
            for (co, cn, bk) in segs:
                w0 = [tn, late["wB"]] + bank_free[bk]
                tk = None
                for k in range(8):
                    tk = P.op("pe", lambda e, k=k, co=co, cn=cn, bk=bk: e.matmul(
                        out=PS[bk][:, 0:cn], lhsT=XNTB[:, bs, k, :], rhs=WIN[:, k, co:co + cn],
                        start=(k == 0), stop=(k == 7)), waits=w0 if k == 0 else (), inc=(k == 7))
                toks.append(tk)
            qv = QRAW[:, bs, :].rearrange("p (h d) -> p h d", d=64)
            kv_ = KRAW[:, bs, :].rearrange("p (h d) -> p h d", d=64)
            uq = U[:, us, 0:512].rearrange("p (h d) -> p h d", d=64)
            uk = U[:, us, 512:640].rearrange("p (h d) -> p h d", d=64)
            cosv = COS[:, t, :]
            sinv = SIN[:, t, :]
            tb0 = P.op("act", lambda e: e.activation(out=KRAW[:, bs, :], in_=PS[2][:, 0:128], func=AF.Copy),
                       waits=[toks[1]])
            ta0 = P.op("act", lambda e: e.activation(out=QRAW[:, bs, :], in_=PS[1][:, 0:512], func=AF.Copy),
                       waits=[toks[0], c_tok])
            ta = P.op("act", lambda e: e.activation(out=U[:, us, 0:512], in_=PS[1][:, 0:512], func=AF.Copy))
            tb = P.op("act", lambda e: e.activation(out=U[:, us, 512:640], in_=PS[2][:, 0:128], func=AF.Copy))
            last_rope = None
            for (src, dst, nh_, hoff) in ((qv, uq, 8, 0), (kv_, uk, 2, 8)):
                r1 = RT[:, bs, hoff:hoff + nh_, :]
                r2 = RT2[:, bs, hoff:hoff + nh_, :]
                a1 = P.op("dve", lambda e, src=src, r1=r1, nh_=nh_: e.tensor_tensor(
                    out=r1, in0=src[:, :, 0:16], in1=bc(cosv, 1, nh_), op=ALU.mult), waits=[ta0, tb0])
                a2 = P.op("dve", lambda e, src=src, r2=r2, nh_=nh_: e.tensor_tensor(
                    out=r2[:, :, 0:8], in0=src[:, :, 8:16], in1=bc(sinv[:, 0:8], 1, nh_), op=ALU.mult))
                a3 = P.op("dve", lambda e, src=src, r2=r2, nh_=nh_: e.tensor_tensor(
                    out=r2[:, :, 8:16], in0=src[:, :, 0:8], in1=bc(sinv[:, 8:16], 1, nh_), op=ALU.mult))
                a4 = P.op("dve", lambda e, dst=dst, r1=r1, r2=r2: e.tensor_tensor(
                    out=dst[:, :, 0:16], in0=r1, in1=r2, op=ALU.add), waits=[a1, a2, a3, ta, tb])
                last_rope = a4
            tv0 = P.op("act", lambda e: e.activation(out=VAUG[:, us, 0, 0:64], in_=PS[2][:, 128:192], func=AF.Copy),
                       waits=[toks[1], t_init])
            tv1 = P.op("act", lambda e: e.activation(out=VAUG[:, us, 1, 64:128], in_=PS[2][:, 192:256], func=AF.Copy))
            tp0 = P.op("act", lambda e: e.activation(out=U[:, us, 768:1024], in_=PS[2][:, 256:512], func=AF.Copy))
            tp1 = P.op("act", lambda e: e.activation(out=U[:, us, 1024:1280], in_=PS[3][:, 0:256], func=AF.Copy),
                       waits=[toks[2]])
            bank_free[1] = [ta]
            bank_free[2] = [tb, tp0]
            bank_free[3] = [tp1]
            w_tok[t] = (last_rope, [tv1, tp1, tp0])

        def QKT(t):
            us = t % 4
            qs = t % 3
            last_rope, others = w_tok[t]
            tt = None
            for c5 in range(5):
                tt = P.op("pe", lambda e, c5=c5: e.transpose(out=PST[:, c5, :], in_=U[:, us, c5 * 128:(c5 + 1) * 128],
                                                             identity=IDB[:]),
                          waits=[last_rope] + bank_free["T"] if c5 == 0 else (), inc=(c5 == 4))
            tq = P.op("act", lambda e: e.activation(out=QT[:, qs, :, :], in_=PST[:, 0:4, :], func=AF.Copy), waits=[tt])
            tk_ = P.op("act", lambda e: e.activation(out=KT[:, us, :], in_=PST[:, 4, :], func=AF.Copy))
            bank_free["T"] = [tq, tk_]
            u_done[t] = [tq, tk_, last_rope] + others

        mx = {}

        def Sc2(i):
            js = [j for j in (i - 1, i, i + 1) if 0 <= j < NT_ALL]
            need = []
            for j in js:
                need += u_done[j]
            ps_ = i % 2
            qs = i % 3
            sb_ = {0: [0, 1, 2], 1: [3, 4, 5]}
            tsc = {}
            first = True
            for bi, j in enumerate(js):
                for kv in range(2):
                    r0, r1 = kv * 64, (kv + 1) * 64
                    bk = sb_[kv][bi]
                    diag = (j == i)
                    w0 = (need + [c_tok] if first else []) + bank_free[bk]
                    first = False
                    tsc[(kv, bi)] = P.op("pe", lambda e, bk=bk, j=j, r0=r0, r1=r1, diag=diag: e.matmul(
                        out=PS[bk][:], lhsT=KT[r0:r1, j % 4, :], rhs=QT[r0:r1, qs, :, :],
                        start=True, stop=diag), waits=w0, inc=diag)
            for bi, j in enumerate(js):
                if j == i:
                    continue
                which = 0 if j < i else 1
                for kv in range(2):
                    bk = sb_[kv][bi]
                    tsc[(kv, bi)] = P.op("pe", lambda e, bk=bk, which=which: e.matmul(
                        out=PS[bk][:], lhsT=IDB[:], rhs=NEGM[:, which, :], start=False, stop=True), inc=True)
            for kv in range(2):
                texp = []
                for bi, j in enumerate(js):
                    bk = sb_[kv][bi]
                    te = P.op("act", lambda e, bk=bk, bi=bi, kv=kv: e.activation(
                        out=PT[:, ps_, kv, bi, :], in_=PS[bk][:], func=AF.Exp, scale=0.125), waits=[tsc[(kv, bi)]])
                    bank_free[bk] = [te]
                    texp.append(te)
                mx[(i, kv)] = texp

        def Vp(i, kv):
            js = [j for j in (i - 1, i, i + 1) if 0 <= j < NT_ALL]
            texp = mx[(i, kv)]
            pb = 0 if kv == 0 else 4
            ps_ = i % 2
            bs = i % 2
            tpv = None
            for bi, j in enumerate(js):
                tpv = P.op("pe", lambda e, j=j, bi=bi: e.matmul(
                    out=PS[pb][:], lhsT=VAUG[:, j % 4, kv, :], rhs=PT[:, ps_, kv, bi, :],
                    start=(bi == 0), stop=False),
                    waits=texp + bank_free[pb] + [t_es, c_tok] if bi == 0 else (), inc=False)
            tpv = P.op("pe", lambda e: e.matmul(out=PS[pb][:], lhsT=ONEROW[0:1, kv, :], rhs=ESROW[0:1, kv, :],
                                                start=False, stop=True), inc=True)
            d0, d1 = (0, 64) if kv == 0 else (64, 128)
            e0, e1 = (64, 128) if kv == 0 else (0, 64)
            rv = RDEN[d0:d1, :].rearrange("p (h q) -> p h q", h=4)
            n1 = P.op("dve", lambda e: e.tensor_copy(out=RDEN[d0:d1, :], in_=PS[pb][e0:e1, :]), waits=[tpv])
            n2 = P.op("act", lambda e: e.activation(out=RDEN[d0:d1, :], in_=RDEN[d0:d1, :], func=AF.Ln), waits=[n1])
            n4 = P.op("act", lambda e: e.activation(out=RDEN[d0:d1, :], in_=RDEN[d0:d1, :], func=AF.Exp, scale=-1.0),
                      waits=[n2])
            mx[(i, "v", kv)] = (n1, n4)

        def Vn(i, kv):
            pb = 0 if kv == 0 else 4
            bs = i % 2
            d0, d1 = (0, 64) if kv == 0 else (64, 128)
            rv = RDEN[d0:d1, :].rearrange("p (h q) -> p h q", h=4)
            n1, n4 = mx[(i, "v", kv)]
            n5 = P.op("dve", lambda e: e.tensor_tensor(out=MIXT[d0:d1, bs, 0:4, :],
                                                       in0=PS[pb][d0:d1, :].rearrange("p (h q) -> p h q", h=4),
                                                       in1=rv, op=ALU.mult), waits=[n4])
            bank_free[pb] = [n1, n5]
            mx[(i, "n", kv)] = n5

        def Pd(i):
            js = [j for j in (i - 1, i, i + 1) if 0 <= j < NT_ALL]
            need = []
            for j in js:
                need += u_done[j]
            tpd = None
            first = True
            for g in range(4):
                for bi, j in enumerate(js):
                    if j < i:
                        v = 0
                    elif j > i:
                        v = 2
                    else:
                        v = 3 if i == 0 else (4 if i == NT_ALL - 1 else 1)
                    tpd = P.op("pe", lambda e, g=g, j=j, v=v, bi=bi: e.matmul(
                        out=PS[6][:, g * 128:(g + 1) * 128], lhsT=U[:, j % 4, 768 + g * 128:768 + (g + 1) * 128],
                        rhs=BPOOL[:, g * 5 + v, :], start=(bi == 0), stop=(bi == len(js) - 1)),
                        waits=need + bank_free[6] + [c_tok] if first else (),
                        inc=(g == 3 and bi == len(js) - 1))
                    first = False
            td_ = P.op("act", lambda e: e.activation(out=DT[:].rearrange("p g t -> p (g t)"), in_=PS[6][:], func=AF.Copy),
                       waits=[tpd])
            bank_free[6] = [td_]
            mx[(i, "d")] = td_

        def Py(i):
            bs = i % 2
            td_ = mx[(i, "d")]
            tpy = None
            for g in range(4):
                tpy = P.op("pe", lambda e, g=g: e.matmul(
                    out=PS[6][:, g * 128:(g + 1) * 128], lhsT=WPOOL[:, g, :], rhs=DT[:, g, :], start=True, stop=True),
                    waits=[td_, late["wB"]] + bank_free[6] if g == 0 else (), inc=(g == 3))
            ty = P.op("dve", lambda e: e.tensor_tensor(out=MIXT[:, bs, 4:8, :],
                                                       in0=PS[6][:].rearrange("p (g t) -> p g t", g=4),
                                                       in1=bc(PSCALE[:], 2, 128), op=ALU.mult), waits=[tpy, c_tok])
            bank_free[6] = [ty]
            mx[(i, "y")] = ty

        def Op(i):
            s = i % HS
            bs = i % 2
            for nh, bk in ((0, 5), (1, 6)):
                w0 = [mx[(i, "n", 0)], mx[(i, "n", 1)], mx[(i, "y")], late["wB"]] + bank_free[bk]
                to = None
                for c in range(8):
                    to = P.op("pe", lambda e, c=c, bk=bk, nh=nh: e.matmul(
                        out=PS[bk][:], lhsT=MIXT[:, bs, c, :], rhs=WOUT[:, c, nh * 512:(nh + 1) * 512],
                        start=(c == 0), stop=(c == 7)), waits=w0 if c == 0 else (), inc=(c == 7))
                te = P.op("dve", lambda e, bk=bk, nh=nh: e.tensor_tensor(
                    out=H[:, s, nh * 512:(nh + 1) * 512], in0=PS[bk][:], in1=H[:, s, nh * 512:(nh + 1) * 512],
                    op=ALU.add), waits=[to, h_ready[i]])
                bank_free[bk] = [te]
                h_ready[i] = te

        RING_N = (XN, "xn", 2, 0)
        RING_C = (XN, "xc", 2, 2)
        RING_A = (XN, "xa", 4, 0)
        xc_free = xn_free

        def stage_b_pro(b_tiles, w_list, na, done):
            pro = [t for t in w_list if t <= b_tiles[0] + 1]
            for t in pro:
                if t in done or t not in h_ready:
                    continue
                done.add(t)
                na[t] = Na(t, RING_N, xn_free)
                tn = Nt(na[t], 1, XNTB[:, t % 2, :, :])
                Wp(t, tn)
                QKT(t)

        def stage_b(p, b_tiles, w_list, na, done):
            xt_c = {}
            ca = {}
            wl = list(w_list)
            stage_b_pro(b_tiles, wl, na, done)
            rest = [t for t in wl if t > b_tiles[0] + 1]
            if rest:
                na[rest[0]] = Na(rest[0], RING_N, xn_free)
            for k in b_tiles:
                sqn = sqc = None
                if (k + 3) in rest:
                    sqn = Na_sq(k + 3, "dve")
                if k - 1 >= b_tiles[0]:
                    sqc = Na_sq(k - 1, "dve")
                Sc2(k)
                Pd(k)
                tn = None
                if (k + 2) in rest:
                    tn = Nt(na[k + 2], 1, XNTB[:, (k + 2) % 2, :, :])
                Vp(k, 0)
                Vp(k, 1)
                Py(k)
                Vn(k, 0)
                Vn(k, 1)
                if sqn is not None:
                    na[k + 3] = Na_rest(k + 3, RING_N, xn_free, sqn, copy_eng="dve")
                if sqc is not None:
                    ca[k - 1] = Na_rest(k - 1, RING_C, xc_free, sqc, copy_eng="dve")
                if (k + 2) in rest:
                    Wp(k + 2, tn)
                Op(k)
                if (k + 2) in rest:
                    QKT(k + 2)
                if k - 1 >= b_tiles[0]:
                    li = k - 1 - b_tiles[0]
                    xt_c[k - 1] = Nt(ca[k - 1], 2, XNT[:, :, li * 128:(li + 1) * 128], bank=2)
            kl = b_tiles[-1]
            ca[kl] = Na(kl, RING_C, xc_free)
            li = kl - b_tiles[0]
            xt_c[kl] = Nt(ca[kl], 2, XNT[:, :, li * 128:(li + 1) * 128])
            return [xt_c[t] for t in b_tiles]

        def final(t):
            s = t % HS
            col = state["st"] * 4
            state["st"] += 1
            t1 = P.op("act", lambda e: e.activation(out=JUNK[:], in_=H[:, s, :], func=AF.Square,
                                                    accum_out=ST[:, col:col + 1]), waits=[h_ready[t], t_init])
            t3 = P.op("act", lambda e: e.activation(out=ST[:, col + 2:col + 3], in_=ST[:, col:col + 1], func=AF.Ln,
                                                    scale=1.0 / D, bias=EPS[:, 0:1]), waits=[t1])
            t4 = P.op("act", lambda e: e.activation(out=ST[:, col + 3:col + 4], in_=ST[:, col + 2:col + 3],
                                                    func=AF.Exp, scale=-0.5), waits=[t3])
            t5 = P.op("dve", lambda e: e.scalar_tensor_tensor(out=H[:, s, :], in0=H[:, s, :],
                                                              scalar=ST[:, col + 3:col + 4], in1=GFIN[:],
                                                              op0=ALU.mult, op1=ALU.mult), waits=[t4, c_tok])
            store_tok[t] = P.dma("pool", out_d[t * 128:(t + 1) * 128, :], H[:, s, :], f"o{s}", waits=[t5])

        a_lists = []
        nxt = 0
        for p in range(NPASS):
            a_hi = min(NT_ALL, (p + 1) * TPP + 1)
            a_lists.append(list(range(nxt, a_hi)))
            nxt = a_hi
        loaded = set()

        def ensure_load(t):
            if t not in loaded and t < NT_ALL:
                loaded.add(t)
                load_x(t)

        xnt_free = None
        na_prev = {}
        pending_fin = []
        for p in range(NPASS):
            b_tiles = list(range(p * TPP, (p + 1) * TPP))
            a_tiles = a_lists[p]
            next_a = a_lists[p + 1] if p + 1 < NPASS else []
            for t in a_tiles:
                ensure_load(t)
            na = dict(na_prev)
            for t in a_tiles[:4]:
                if t not in na:
                    na[t] = Na(t, RING_A, xn_free, sq_eng=("dve" if t % 2 == 0 else "act"))
            xt = []
            if xnt_free is not None:
                P.wait_only("dve", [xnt_free])
            for li, t in enumerate(a_tiles):
                xt.append(Nt(na[t], 0, XNT[:, :, li * 128:(li + 1) * 128], bank=("T" if li % 2 == 0 else 2)))
                if li + 4 < len(a_tiles):
                    t4_ = a_tiles[li + 4]
                    na[t4_] = Na(t4_, RING_A, xn_free, sq_eng=("dve" if t4_ % 2 == 0 else "act"))
                if li == 3 and pending_fin:
                    pending_fin.pop()()
            if pending_fin:
                pending_fin.pop()()
            if p == 0:
                late_sp()
            nb_ = {}
            done_ = set()
            npro = len([t for t in a_tiles if t <= b_tiles[0] + 1])

            sched = {}
            pro_tiles = [t for t in a_tiles if t <= b_tiles[0] + 1]
            tn_ = {}
            for j, t in enumerate(pro_tiles):
                base = len(pro_tiles) + j

                def s_na(t=t, nb_=nb_, done_=done_):
                    done_.add(t)
                    nb_[t] = Na(t, RING_N, xn_free)

                def s_nt(t=t, nb_=nb_, tn_=tn_):
                    tn_[t] = Nt(nb_[t], 1, XNTB[:, t % 2, :, :])

                def s_w(t=t, tn_=tn_):
                    Wp(t, tn_[t])

                def s_q(t=t):
                    QKT(t)
                sched.setdefault(base, []).append(s_na)
                sched.setdefault(base + 1, []).append(s_nt)
                sched.setdefault(base + 2, []).append(s_w)
                sched.setdefault(base + 4, []).append(s_q)
            last_li = len(a_tiles) - 1
            pending = []

            def cb1(li, sched=sched, last_li=last_li):
                for k_ in sorted(sched):
                    if k_ <= li or li == last_li:
                        for f_ in sched.pop(k_):
                            f_()

            xnt_free = ffn(0, a_tiles, xt, on_last_tile=cb1)
            P.wait_only("dve", [xnt_free])
            xt = stage_b(p, b_tiles, a_tiles, nb_, done_)
            if next_a:
                ensure_load(next_a[0])

            def fin(t):
                final(t)
                if (t + HS) in next_a:
                    ensure_load(t + HS)

            na_next = {}

            def cb(li, b_tiles=b_tiles, next_a=next_a, na_next=na_next):
                if li >= 1:
                    fin(b_tiles[li - 1])
                if li >= 4 and li - 4 < len(next_a):
                    tn_ = next_a[li - 4]
                    ensure_load(tn_)
                    na_next[tn_] = Na(tn_, RING_A, xn_free, sq_eng=("dve" if tn_ % 2 == 0 else "act"))

            xnt_free = ffn(1, b_tiles, xt, on_last_tile=cb)
            if p + 1 < NPASS:
                pending_fin.append(lambda fin=fin, t=b_tiles[-1]: fin(t))
            else:
                fin(b_tiles[-1])
            na_prev = na_next
        P.wait_only("sp", [store_tok[t] for t in range(NT_ALL)])

        block = E(nc.Block())

        @block.tensor
        def _(e):
            P.emit("pe", e, sems)

        @block.scalar
        def _(e):
            P.emit("act", e, sems)

        @block.vector
        def _(e):
            P.emit("dve", e, sems)

        @block.gpsimd
        def _(e):
            P.emit("pool", e, sems)

        @block.sync
        def _(e):
            P.emit("sp", e, sems)
    return nc


def _pool_consts(S=S):
    NT_ALL = S // 128
    wins = (2, 4, 8, 16)
    B = np.zeros((128, 20, 128), np.float64)
    for g, w in enumerate(wins):
        half = w // 2
        for (tile_i, variants) in ((0, {0: 3, 1: 2}), (1, {0: 0, 1: 1, 2: 2}), (NT_ALL - 1, {NT_ALL - 2: 0, NT_ALL - 1: 4})):
            for tl in range(128):
                t = tile_i * 128 + tl
                coef = {}
                for (lo, hi) in ((t - half, t + half - 1), (t - half + 1, t + half)):
                    a = min(max(lo, 0), S)
                    b = min(max(hi + 1, 0), S)
                    for tp in range(a, b):
                        coef[tp] = coef.get(tp, 0.0) + 0.5 / (b - a)
                coef[t] = coef.get(t, 0.0) - 1.0
                for tp, cval in coef.items():
                    j = tp // 128
                    if j not in variants:
                        continue
                    B[tp % 128, g * 5 + variants[j], tl] = cval
    return B.astype(np.float32)


def _consts(S=S):
    NT_ALL = S // 128
    ident = np.eye(128, dtype=np.float32)
    b = np.arange(128)[:, None]
    a = np.arange(128)[None, :]
    m0 = np.where(a > b, NEG, 0.0).astype(np.float32)
    m1 = np.where(b > a, NEG, 0.0).astype(np.float32)
    negm = np.stack([np.tile(m0, (1, 4)), np.tile(m1, (1, 4))], axis=1)
    inv_freq = np.float32(500000.0) ** (-(np.arange(0, 16, 2, dtype=np.float32) / np.float32(16)))
    ang = np.arange(S, dtype=np.float32)[:, None] * inv_freq[None, :].astype(np.float32)
    emb = np.concatenate([ang, ang], axis=-1).astype(np.float32)
    cos = np.cos(emb.astype(np.float64)).astype(np.float32)
    sin = np.sin(emb.astype(np.float64)).astype(np.float32)
    sin_s = sin.copy()
    sin_s[:, 0:8] = -sin[:, 0:8]
    cosT = np.ascontiguousarray(cos.reshape(NT_ALL, 128, 16).transpose(1, 0, 2))
    sinT = np.ascontiguousarray(sin_s.reshape(NT_ALL, 128, 16).transpose(1, 0, 2))
    return ident, np.ascontiguousarray(negm), _pool_consts(S), cosT, sinT


_CACHE = {}


def prep_shared(ffn1_norm, ffn1_w_gate, ffn1_w_up, ffn1_w_down, mix_norm, w_in,
                sink_logits, pool_w, pool_scale, w_out, ffn2_norm, ffn2_w_gate,
                ffn2_w_up, ffn2_w_down, final_norm, S_=S):
    f = lambda a: np.ascontiguousarray(np.asarray(a, dtype=np.float32))
    ident, negm, bpool, cosT, sinT = _consts(S_)

    def gu_layout(wg, wu):
        arr = np.stack([f(wg)[0], f(wu)[0]])
        arr = arr.reshape(2, 8, 128, NCH, 128)
        arr = arr.transpose(3, 2, 0, 1, 4)
        return np.ascontiguousarray(arr).reshape(NCH, 128, 2 * 8 * 128)

    perm_q = []
    for c in range(4):
        perm_q += list(range(c * 64, (c + 1) * 64)) + list(range((4 + c) * 64, (5 + c) * 64))
    perm_q = np.array(perm_q)
    col_perm = np.concatenate([perm_q, np.arange(512, 1280)])
    row_perm = np.concatenate([perm_q, np.arange(512, 1024)])
    win_p = np.ascontiguousarray(f(w_in)[0][:, col_perm])
    wout_p = np.ascontiguousarray(f(w_out)[0][row_perm, :])
    wpool = np.ascontiguousarray(f(pool_w)[0].transpose(1, 0, 2))
    gs = np.stack([f(ffn1_norm)[0], f(mix_norm)[0], f(ffn2_norm)[0]])
    gfm = np.ascontiguousarray(gs.reshape(3, 8, 128).transpose(2, 0, 1))
    gfin = np.ascontiguousarray(np.broadcast_to(f(final_norm)[None, :], (128, D)))
    pscale = np.ascontiguousarray(f(pool_scale)[0].reshape(4, 128).T)
    sk = f(sink_logits)[0]
    sinkb = np.ascontiguousarray(np.concatenate([np.broadcast_to(sk[0:4][None, :], (64, 4)),
                                                 np.broadcast_to(sk[4:8][None, :], (64, 4))], axis=0))
    return {
        "wgu1": gu_layout(ffn1_w_gate, ffn1_w_up), "wgu2": gu_layout(ffn2_w_gate, ffn2_w_up),
        "wd1": f(ffn1_w_down)[0], "wd2": f(ffn2_w_down)[0],
        "win": win_p, "wout": wout_p, "wpool": wpool, "ident": ident, "negm": negm, "bpool": bpool,
        "cosT": cosT, "sinT": sinT, "gfm": gfm, "gfin": gfin, "pscale": pscale, "sinkb": sinkb,
    }


def kernel(x, **params):
    x = np.ascontiguousarray(np.asarray(x, dtype=np.float32))
    if "nc" not in _CACHE:
        _CACHE["nc"] = build_program()
    nc = _CACHE["nc"]
    shared = prep_shared(**params)
    in_maps = []
    for c in range(8):
        m = dict(shared)
        m["x"] = x[c]
        in_maps.append(m)
    res = run_bass_kernel_spmd(nc, in_maps, core_ids=list(range(8)))
    return np.stack([np.asarray(r["out"], dtype=np.float32) for r in res.results], axis=0)
```

```python
import numpy as np
from contextlib import ExitStack

import concourse.bass as bass
import concourse.mybir as mybir
from concourse.bass_utils import run_bass_kernel_spmd

F32 = mybir.dt.float32
BF16 = mybir.dt.bfloat16
AF = mybir.ActivationFunctionType
ALU = mybir.AluOpType

S = 4096
D = 1024
DFF = 2816
NT_ALL = 32
NCH = 22
BLOCKS = [list(range(0, 6)), list(range(6, 12)), list(range(12, 17)), list(range(17, 22))]
NPASS = 4
TPP = 8
HS = 10
NEG = -30000.0


def bc(ap, axis, n):
    shp = list(ap.shape)
    a = ap.unsqueeze(axis)
    shp.insert(axis, n)
    return a.broadcast_to(shp)


class Plan:
    ENGS = ("pe", "act", "dve", "pool", "sp")

    def __init__(self):
        self.ops = {e: [] for e in self.ENGS}
        self.cnt = {}
        self.seen = {e: {} for e in self.ENGS}

    def _filter(self, eng, waits):
        w = []
        for t in waits:
            if t is None:
                continue
            if isinstance(t, list):
                for tt in t:
                    waits.append(tt)
                continue
            key, val = t
            if self.seen[eng].get(key, 0) >= val:
                continue
            self.seen[eng][key] = val
            w.append((key, val))
        return w

    def op(self, eng, fn, waits=(), inc=True):
        w = self._filter(eng, list(waits))
        tok = None
        incinfo = None
        if inc:
            self.cnt[eng] = self.cnt.get(eng, 0) + 1
            tok = (eng, self.cnt[eng])
            incinfo = (eng, 1)
        self.ops[eng].append((w, fn, incinfo))
        return tok

    def dma(self, eng, out, in_, key, waits=()):
        w = self._filter(eng, list(waits))
        self.cnt[key] = self.cnt.get(key, 0) + 16
        tok = (key, self.cnt[key])
        self.ops[eng].append((w, lambda e, o=out, i=in_: e.dma_start(out=o, in_=i), (key, 16)))
        return tok

    def wait_only(self, eng, waits):
        w = self._filter(eng, list(waits))
        if w:
            self.ops[eng].append((w, None, None))

    def emit(self, eng, e, sems):
        for (w, fn, incinfo) in self.ops[eng]:
            if fn is None:
                for (key, val) in w:
                    e.wait_ge(sems[key], val)
                continue
            for (key, val) in w[:-1]:
                e.wait_ge(sems[key], val)
            ins = fn(e)
            if w:
                ins._wait_ge(sems[w[-1][0]], w[-1][1])
            if incinfo is not None:
                ins.then_inc(sems[incinfo[0]], incinfo[1])


def build_program(S=S):
    NT_ALL = S // 128
    NPASS = NT_ALL // TPP
    nc = bass.Bass("TRN2", target_bir_lowering=False)
    dt = nc.dram_tensor
    x_d = dt("x", [S, D], F32, kind="ExternalInput").ap()
    wgu_d = [dt("wgu1", [NCH, 128, 2 * 8 * 128], F32, kind="ExternalInput").ap(),
             dt("wgu2", [NCH, 128, 2 * 8 * 128], F32, kind="ExternalInput").ap()]
    wd_d = [dt("wd1", [DFF, D], F32, kind="ExternalInput").ap(),
            dt("wd2", [DFF, D], F32, kind="ExternalInput").ap()]
    win_d = dt("win", [D, 1280], F32, kind="ExternalInput").ap()
    wout_d = dt("wout", [D, D], F32, kind="ExternalInput").ap()
    wpool_d = dt("wpool", [128, 4, 128], F32, kind="ExternalInput").ap()
    ident_d = dt("ident", [128, 128], F32, kind="ExternalInput").ap()
    negm_d = dt("negm", [128, 2, 512], F32, kind="ExternalInput").ap()
    bpool_d = dt("bpool", [128, 20, 128], F32, kind="ExternalInput").ap()
    cos_d = dt("cosT", [128, NT_ALL, 16], F32, kind="ExternalInput").ap()
    sin_d = dt("sinT", [128, NT_ALL, 16], F32, kind="ExternalInput").ap()
    gfm_d = dt("gfm", [128, 3, 8], F32, kind="ExternalInput").ap()
    gfin_d = dt("gfin", [128, D], F32, kind="ExternalInput").ap()
    pscale_d = dt("pscale", [128, 4], F32, kind="ExternalInput").ap()
    sink_d = dt("sinkb", [128, 4], F32, kind="ExternalInput").ap()
    out_d = dt("out", [S, D], F32, kind="ExternalOutput").ap()

    with ExitStack() as es:
        E = es.enter_context
        sb = lambda name, shape, dty: E(nc.sbuf_tensor(name, shape, dty))
        H = sb("H", [128, HS, D], F32)
        XNT = sb("XNT", [128, 8, 9 * 128], BF16)
        HID = sb("HID", [128, 6, 9 * 128], BF16)
        WD = sb("WD", [128, 6, D], BF16)
        WGU = sb("WGU", [128, 4, 2 * 8 * 128], BF16)
        SG = sb("SG", [128, 2, 512], BF16)
        JUNK = sb("JUNK", [128, D], BF16)
        JUNK2 = sb("JUNK2", [128, D], BF16)
        XN = sb("XN", [128, 4, D], BF16)
        EPS = sb("EPS", [128, 1], F32)
        ST = sb("ST", [128, 4 * 4 * NT_ALL + 64], F32)
        WIN = sb("WIN", [128, 8, 1280], BF16)
        WOUT = sb("WOUT", [128, 8, D], BF16)
        WPOOL = sb("WPOOL", [128, 4, 128], BF16)
        IDB = sb("IDB", [128, 128], BF16)
        NEGM = sb("NEGM", [128, 2, 512], BF16)
        BPOOL = sb("BPOOL", [128, 20, 128], BF16)
        COS = sb("COS", [128, NT_ALL, 16], F32)
        SIN = sb("SIN", [128, NT_ALL, 16], F32)
        GFM = sb("GFM", [128, 3, 8], F32)
        GFIN = sb("GFIN", [128, D], F32)
        PSCALE = sb("PSCALE", [128, 4], F32)
        SINKB = sb("SINKB", [128, 4], F32)
        ESINK = sb("ESINK", [128, 4], F32)
        ESROW = sb("ESROW", [1, 2, 512], BF16)
        ONEROW = sb("ONEROW", [1, 2, 128], BF16)
        U = sb("U", [128, 4, 1280], BF16)
        VAUG = sb("VAUG", [128, 4, 2, 128], BF16)
        QT = sb("QT", [128, 3, 4, 128], BF16)
        KT = sb("KT", [128, 4, 128], BF16)
        XNTB = sb("XNTB", [128, 2, 8, 128], BF16)
        MIXT = sb("MIXT", [128, 2, 8, 128], BF16)
        RDEN = sb("RDEN", [128, 512], F32)
        DT = sb("DT", [128, 4, 128], BF16)
        KRAW = sb("KRAW", [128, 2, 128], F32)
        QRAW = sb("QRAW", [128, 2, 512], F32)
        RT = sb("RT", [128, 2, 10, 16], F32)
        RT2 = sb("RT2", [128, 2, 10, 16], F32)
        PS = [E(nc.psum_tensor(f"ps{i}", [128, 512], F32)) for i in range(7)]
        PST = E(nc.psum_tensor("pst", [128, 8, 128], BF16))
        PSTF = PST[:].rearrange("p a b -> p (a b)").bitcast(F32)
        PT = HID[:].rearrange("p a b -> p (a b)")[:, 0:6144].rearrange("p (s k b n) -> p s k b n", s=2, k=2, b=3)

        P = Plan()
        sem_keys = ["pe", "act", "dve", "pool", "c", "cg", "c2", "cg2", "wd", "wB"] + [f"x{i}" for i in range(HS)] + \
                   [f"o{i}" for i in range(HS)] + [f"wgu{i}" for i in range(4)]
        sems = {k: E(nc.semaphore(k)) for k in sem_keys}

        bank_free = {i: [] for i in range(7)}
        bank_free["T"] = []
        h_ready = {}
        store_tok = {}
        state = {"gu": 0, "pair": 0, "sgs": 0, "dn": 0, "xn": 0, "xc": 0, "xa": 0, "st": 0}
        gu_free = [None] * 4
        sg_free = [None] * 2
        xn_free = [None] * 4
        wd_free = [None]

        P.dma("sp", GFM[:], gfm_d, "c")
        P.dma("sp", PSCALE[:], pscale_d, "c")
        c_sp = P.dma("sp", SINKB[:], sink_d, "c")
        tok_idb = P.dma("pool", IDB[:], ident_d, "cg")
        c_tok = [c_sp, tok_idb]
        late = {}

        def late_sp():
            P.dma("sp", COS[:], cos_d, "c2")
            P.dma("sp", SIN[:], sin_d, "c2")
            c_tok.append(P.dma("sp", GFIN[:], gfin_d, "c2"))

        def late_pool(b):
            if b == 1:
                P.dma("pool", WIN[:], win_d.rearrange("(k p) n -> p k n", p=128), "wB")
                late["b1"] = True
            elif b == 2:
                P.dma("pool", WOUT[:], wout_d.rearrange("(k p) n -> p k n", p=128), "wB")
                late["b2"] = True
            elif b == 3:
                P.dma("pool", NEGM[:], negm_d, "cg2")
                c_tok.append(P.dma("pool", BPOOL[:], bpool_d, "cg2"))
                late["wB"] = P.dma("pool", WPOOL[:], wpool_d, "wB")

        t_init = P.op("dve", lambda e: e.memset(ST[:], 0.0))
        t_init = P.op("dve", lambda e: e.memset(EPS[:], 1e-6))
        t_init = P.op("dve", lambda e: e.memset(VAUG[:], 1.0))
        t_es = P.op("act", lambda e: e.activation(out=ESINK[:], in_=SINKB[:], func=AF.Exp), waits=[c_tok])
        t_es1 = P.op("act", lambda e: e.activation(out=ESROW[0:1, 0, :].rearrange("p (h q) -> p h q", h=4),
                                                   in_=bc(ESINK[0:1, :], 2, 128), func=AF.Copy), waits=[t_es])
        t_es = P.op("act", lambda e: e.activation(out=ESROW[0:1, 1, :].rearrange("p (h q) -> p h q", h=4),
                                                  in_=bc(ESINK[64:65, :], 2, 128), func=AF.Copy), waits=[t_es])
        P.op("dve", lambda e: e.memset(ONEROW[0:1, 0, 0:64], 0.0))
        P.op("dve", lambda e: e.memset(ONEROW[0:1, 0, 64:128], 1.0))
        P.op("dve", lambda e: e.memset(ONEROW[0:1, 1, 0:64], 1.0))
        t_one = P.op("dve", lambda e: e.memset(ONEROW[0:1, 1, 64:128], 0.0))
        t_es = [t_es, t_one]

        def load_x(t):
            s = t % HS
            w = [store_tok.get(t - HS)]
            h_ready[t] = P.dma("sp", H[:, s, :], x_d[t * 128:(t + 1) * 128, :], f"x{s}", waits=w)

        def norm_T(t, which, dstT):
            s = t % HS
            col = state["st"] * 4
            state["st"] += 1
            t1 = P.op("act", lambda e: e.activation(out=JUNK[:], in_=H[:, s, :], func=AF.Square,
                                                    accum_out=ST[:, col:col + 1]),
                      waits=[h_ready[t], t_init])
            t2 = P.op("dve", lambda e: e.tensor_scalar(out=ST[:, col + 1:col + 2], in0=ST[:, col:col + 1],
                                                       scalar1=1.0 / D, scalar2=1e-6, op0=ALU.mult, op1=ALU.add),
                      waits=[t1])
            t3 = P.op("act", lambda e: e.activation(out=ST[:, col + 2:col + 3], in_=ST[:, col + 1:col + 2], func=AF.Ln),
                      waits=[t2])
            t4 = P.op("act", lambda e: e.activation(out=ST[:, col + 3:col + 4], in_=ST[:, col + 2:col + 3],
                                                    func=AF.Exp, scale=-0.5), waits=[t3])
            xs = state["xn"] % 2
            state["xn"] += 1
            t5 = P.op("act", lambda e: e.activation(out=XN[:, xs, :], in_=H[:, s, :], func=AF.Copy,
                                                    scale=ST[:, col + 3:col + 4]), waits=[t4, xn_free[xs]])
            t6 = None
            for k in range(8):
                t6 = P.op("pe", lambda e, k=k: e.transpose(out=PST[:, k, :], in_=XN[:, xs, k * 128:(k + 1) * 128],
                                                           identity=IDB[:]),
                          waits=[t5, c_tok] + bank_free["T"] if k == 0 else (), inc=(k == 7))
            xn_free[xs] = t6
            t7 = P.op("dve", lambda e: e.tensor_tensor(out=dstT, in0=PST[:], in1=bc(GFM[:, which, :], 2, 128),
                                                       op=ALU.mult), waits=[t6, c_tok])
            bank_free["T"] = [t7]
            return t7, col + 3

        gu_seq = [(n_, c_) for _p in range(NPASS) for n_ in (0, 1) for c_ in range(NCH)]
        gu_tok = {}
        gu_done = {}

        def plan_gu_load(g):
            if g >= len(gu_seq):
                return
            n_, c_ = gu_seq[g]
            slot = g % 4
            gu_tok[g] = P.dma("pool", WGU[:, slot, :], wgu_d[n_][c_], f"wgu{slot}", waits=[gu_done.get(g - 4)])

        for g0 in range(4):
            plan_gu_load(g0)

        def ffn(n, tiles, xnt_toks, on_last_tile=None):
            NTl = len(tiles)
            groups = [(o, min(512, NTl * 128 - o)) for o in range(0, NTl * 128, 512)]
            last_gate_up = None
            for b, chunks in enumerate(BLOCKS):
                nb = len(chunks)

                def plan_wd(chunks=chunks, nb=nb):
                    return P.dma("pool", WD[:, 0:nb, :],
                                 wd_d[n][chunks[0] * 128:(chunks[-1] + 1) * 128, :].rearrange("(c p) n -> p c n", p=128),
                                 "wd", waits=[wd_free[0]])
                tok_wd = None
                if b != 0:
                    tok_wd = plan_wd()
                if "wB" not in late and b >= 1:
                    late_pool(b)
                th = None
                for ci, c in enumerate(chunks):
                    g = state["gu"]
                    slot = g % 4
                    state["gu"] += 1
                    assert gu_seq[g] == (n, c)
                    tok_gu = gu_tok[g]
                    wv = WGU[:, slot, :].rearrange("p (g k j) -> p g k j", g=2, k=8)
                    tu = None
                    for (o, nn) in groups:
                        pr = state["pair"] % 2
                        state["pair"] += 1
                        bg, bu = 2 * pr, 2 * pr + 1
                        lastli = (o + nn) // 128 - 1
                        w0 = [tok_gu, xnt_toks[lastli]] + bank_free[bg] + bank_free[bu]
                        tg = None
                        for k in range(8):
                            tg = P.op("pe", lambda e, k=k, bg=bg, o=o, nn=nn, wv=wv: e.matmul(
                                out=PS[bg][:, 0:nn], lhsT=wv[:, 0, k, :], rhs=XNT[:, k, o:o + nn],
                                start=(k == 0), stop=(k == 7)), waits=w0 if k == 0 else (), inc=(k == 7))
                        for k in range(8):
                            tu = P.op("pe", lambda e, k=k, bu=bu, o=o, nn=nn, wv=wv: e.matmul(
                                out=PS[bu][:, 0:nn], lhsT=wv[:, 1, k, :], rhs=XNT[:, k, o:o + nn],
                                start=(k == 0), stop=(k == 7)), inc=(k == 7))
                        ss = state["sgs"] % 2
                        state["sgs"] += 1
                        ts = P.op("act", lambda e, bg=bg, nn=nn, ss=ss: e.activation(
                            out=SG[:, ss, 0:nn], in_=PS[bg][:, 0:nn], func=AF.Silu), waits=[tg, sg_free[ss]])
                        th = P.op("dve", lambda e, bu=bu, nn=nn, ss=ss, ci=ci, o=o: e.tensor_tensor(
                            out=HID[:, ci, o:o + nn], in0=PS[bu][:, 0:nn], in1=SG[:, ss, 0:nn], op=ALU.mult),
                            waits=[tu, ts])
                        bank_free[bg] = [ts]
                        bank_free[bu] = [th]
                        sg_free[ss] = th
                    gu_done[g] = tu
                    plan_gu_load(g + 4)
                    if b == 0 and ci == 1:
                        tok_wd = plan_wd()
                    last_gate_up = tu
                td = None
                four = not (on_last_tile is not None and b == len(BLOCKS) - 1 and n == 0)
                for li, t in enumerate(tiles):
                    s = t % HS
                    if four:
                        pairs = [(4, PS[4][:]), (5, PS[5][:])] if li % 2 == 0 else [(6, PS[6][:]), ("T", PSTF)]
                    else:
                        pairs = None
                    for nh in range(2):
                        if four:
                            bd, bap = pairs[nh]
                            w0 = ([tok_wd, th] + bank_free[pairs[0][0]] + bank_free[pairs[1][0]]) if nh == 0 else []
                        else:
                            bd = 4 + (state["dn"] % 2)
                            bap = PS[bd][:]
                            state["dn"] += 1
                            w0 = [tok_wd, th] + bank_free[bd]
                        for ci in range(nb):
                            td = P.op("pe", lambda e, ci=ci, bap=bap, li=li, nh=nh, nb=nb: e.matmul(
                                out=bap, lhsT=HID[:, ci, li * 128:(li + 1) * 128],
                                rhs=WD[:, ci, nh * 512:(nh + 1) * 512], start=(ci == 0), stop=(ci == nb - 1)),
                                waits=w0 if ci == 0 else (), inc=(ci == nb - 1))
                        te = P.op("dve", lambda e, bap=bap, s=s, nh=nh: e.scalar_tensor_tensor(
                            out=H[:, s, nh * 512:(nh + 1) * 512], in0=bap, scalar=0.5,
                            in1=H[:, s, nh * 512:(nh + 1) * 512], op0=ALU.mult, op1=ALU.add),
                            waits=[td, h_ready[t]])
                        bank_free[bd] = [te]
                        h_ready[t] = te
                    if on_last_tile is not None and b == len(BLOCKS) - 1:
                        on_last_tile(li)
                wd_free[0] = td
            return last_gate_up

        u_done = {}
        w_tok = {}

        def Na_sq(t, sq_eng="act"):
            s = t % HS
            col = state["st"] * 4
            state["st"] += 1
            if sq_eng == "act":
                t1 = P.op("act", lambda e: e.activation(out=JUNK[:], in_=H[:, s, :], func=AF.Square,
                                                        accum_out=ST[:, col:col + 1]),
                          waits=[h_ready[t], t_init])
            else:
                t1 = P.op("dve", lambda e: e.scalar_tensor_tensor(out=JUNK2[:], in0=H[:, s, :], scalar=1.0,
                                                                  in1=H[:, s, :], op0=ALU.mult, op1=ALU.mult,
                                                                  accum_out=ST[:, col:col + 1]),
                          waits=[h_ready[t], t_init, state.get("junk2")])
                state["junk2"] = t1
            return (col, t1)

        def Na_rest(t, ring, free_list, sq, copy_eng="act"):
            s = t % HS
            col, t1 = sq
            xs = ring[3] + state[ring[1]] % ring[2]
            state[ring[1]] += 1
            buf = ring[0]
            t3 = P.op("act", lambda e: e.activation(out=ST[:, col + 2:col + 3], in_=ST[:, col:col + 1], func=AF.Ln,
                                                    scale=1.0 / D, bias=EPS[:, 0:1]), waits=[t1])
            t4 = P.op("act", lambda e: e.activation(out=ST[:, col + 3:col + 4], in_=ST[:, col + 2:col + 3],
                                                    func=AF.Exp, scale=-0.5), waits=[t3])
            if copy_eng == "act":
                t5 = P.op("act", lambda e: e.activation(out=buf[:, xs, :], in_=H[:, s, :], func=AF.Copy,
                                                        scale=ST[:, col + 3:col + 4]), waits=[t4, free_list[xs]])
            else:
                t5 = P.op(copy_eng, lambda e: e.tensor_scalar(out=buf[:, xs, :], in0=H[:, s, :],
                                                            scalar1=ST[:, col + 3:col + 4], scalar2=None,
                                                            op0=ALU.mult), waits=[t4, free_list[xs]])
            return (t5, xs, buf, free_list)

        def Na(t, ring, free_list, sq_eng="act"):
            return Na_rest(t, ring, free_list, Na_sq(t, sq_eng))

        PS2B = PS[2][:].bitcast(BF16).rearrange("p (k t) -> p k t", k=8)

        def Nt(na, which, dstT, bank="T"):
            t5, xs, buf, free_list = na
            pt = PST if bank == "T" else PS2B
            t6 = None
            for k in range(8):
                t6 = P.op("pe", lambda e, k=k: e.transpose(out=pt[:, k, :], in_=buf[:, xs, k * 128:(k + 1) * 128],
                                                           identity=IDB[:]),
                          waits=[t5, c_tok] + bank_free[bank] if k == 0 else (), inc=(k == 7))
            free_list[xs] = t6
            t7 = P.op("dve", lambda e: e.tensor_tensor(out=dstT, in0=pt[:], in1=bc(GFM[:, which, :], 2, 128),
                                                       op=ALU.mult), waits=[t6, c_tok])
            bank_free[bank] = [t7]
            return t7

        def Wp(t, tn):
            us = t % 4
            bs = t % 2
            qs = t % 3
            segs = [(0, 512, 1), (512, 512, 2), (1024, 256, 3)]
            toks = []
            for (co, cn, bk) in segs:
                w0 = [tn, late["wB"]] + bank_free[bk]
                tk = None
                for k in range(8):
                    tk = P.op("pe", lambda e, k=k, co=co, cn=cn, bk=bk: e.matmul(
                        out=PS[bk][:, 0:cn], lhsT=XNTB[:, bs, k, :], rhs=WIN[:, k, co:co + cn],
                        start=(k == 0), stop=(k == 7)), waits=w0 if k == 0 else (), inc=(k == 7))
                toks.append(tk)
            qv = QRAW[:, bs, :].rearrange("p (h d) -> p h d", d=64)
            kv_ = KRAW[:, bs, :].rearrange("p (h d) -> p h d", d=64)
            uq = U[:, us, 0:512].rearrange("p (h d) -> p h d", d=64)
            uk = U[:, us, 512:640].rearrange("p (h d) -> p h d", d=64)
            cosv = COS[:, t, :]
            sinv = SIN[:, t, :]
            tb0 = P.op("act", lambda e: e.activation(out=KRAW[:, bs, :], in_=PS[2][:, 0:128], func=AF.Copy),
                       waits=[toks[1]])
            ta0 = P.op("act", lambda e: e.activation(out=QRAW[:, bs, :], in_=PS[1][:, 0:512], func=AF.Copy),
                       waits=[toks[0], c_tok])
            ta = P.op("act", lambda e: e.activation(out=U[:, us, 0:512], in_=PS[1][:, 0:512], func=AF.Copy))
            tb = P.op("act", lambda e: e.activation(out=U[:, us, 512:640], in_=PS[2][:, 0:128], func=AF.Copy))
            last_rope = None
            for (src, dst, nh_, hoff) in ((qv, uq, 8, 0), (kv_, uk, 2, 8)):
                r1 = RT[:, bs, hoff:hoff + nh_, :]
                r2 = RT2[:, bs, hoff:hoff + nh_, :]
                a1 = P.op("dve", lambda e, src=src, r1=r1, nh_=nh_: e.tensor_tensor(
                    out=r1, in0=src[:, :, 0:16], in1=bc(cosv, 1, nh_), op=ALU.mult), waits=[ta0, tb0])
                a2 = P.op("dve", lambda e, src=src, r2=r2, nh_=nh_: e.tensor_tensor(
                    out=r2[:, :, 0:8], in0=src[:, :, 8:16], in1=bc(sinv[:, 0:8], 1, nh_), op=ALU.mult))
                a3 = P.op("dve", lambda e, src=src, r2=r2, nh_=nh_: e.tensor_tensor(
                    out=r2[:, :, 8:16], in0=src[:, :, 0:8], in1=bc(sinv[:, 8:16], 1, nh_), op=ALU.mult))
                a4 = P.op("dve", lambda e, dst=dst, r1=r1, r2=r2: e.tensor_tensor(
                    out=dst[:, :, 0:16], in0=r1, in1=r2, op=ALU.add), waits=[a1, a2, a3, ta, tb])
                last_rope = a4
            tv0 = P.op("act", lambda e: e.activation(out=VAUG[:, us, 0, 0:64], in_=PS[2][:, 128:192], func=AF.Copy),
                       waits=[toks[1], t_init])
            tv1 = P.op("act", lambda e: e.activation(out=VAUG[:, us, 1, 64:128], in_=PS[2][:, 192:256], func=AF.Copy))
            tp0 = P.op("act", lambda e: e.activation(out=U[:, us, 768:1024], in_=PS[2][:, 256:512], func=AF.Copy))
            tp1 = P.op("act", lambda e: e.activation(out=U[:, us, 1024:1280], in_=PS[3][:, 0:256], func=AF.Copy),
                       waits=[toks[2]])
            bank_free[1] = [ta]
            bank_free[2] = [tb, tp0]
            bank_free[3] = [tp1]
            w_tok[t] = (last_rope, [tv1, tp1, tp0])

        def QKT(t):
            us = t % 4
            qs = t % 3
            last_rope, others = w_tok[t]
            tt = None
            for c5 in range(5):
                tt = P.op("pe", lambda e, c5=c5: e.transpose(out=PST[:, c5, :], in_=U[:, us, c5 * 128:(c5 + 1) * 128],
                                                             identity=IDB[:]),
                          waits=[last_rope] + bank_free["T"] if c5 == 0 else (), inc=(c5 == 4))
            tq = P.op("act", lambda e: e.activation(out=QT[:, qs, :, :], in_=PST[:, 0:4, :], func=AF.Copy), waits=[tt])
            tk_ = P.op("act", lambda e: e.activation(out=KT[:, us, :], in_=PST[:, 4, :], func=AF.Copy))
            bank_free["T"] = [tq, tk_]
            u_done[t] = [tq, tk_, last_rope] + others

        mx = {}

        def Sc2(i):
            js = [j for j in (i - 1, i, i + 1) if 0 <= j < NT_ALL]
            need = []
            for j in js:
                need += u_done[j]
            ps_ = i % 2
            qs = i % 3
            sb_ = {0: [0, 1, 2], 1: [3, 4, 5]}
            tsc = {}
            first = True
            for bi, j in enumerate(js):
                for kv in range(2):
                    r0, r1 = kv * 64, (kv + 1) * 64
                    bk = sb_[kv][bi]
                    diag = (j == i)
                    w0 = (need + [c_tok] if first else []) + bank_free[bk]
                    first = False
                    tsc[(kv, bi)] = P.op("pe", lambda e, bk=bk, j=j, r0=r0, r1=r1, diag=diag: e.matmul(
                        out=PS[bk][:], lhsT=KT[r0:r1, j % 4, :], rhs=QT[r0:r1, qs, :, :],
                        start=True, stop=diag), waits=w0, inc=diag)
            for bi, j in enumerate(js):
                if j == i:
                    continue
                which = 0 if j < i else 1
                for kv in range(2):
                    bk = sb_[kv][bi]
                    tsc[(kv, bi)] = P.op("pe", lambda e, bk=bk, which=which: e.matmul(
                        out=PS[bk][:], lhsT=IDB[:], rhs=NEGM[:, which, :], start=False, stop=True), inc=True)
            for kv in range(2):
                texp = []
                for bi, j in enumerate(js):
                    bk = sb_[kv][bi]
                    te = P.op("act", lambda e, bk=bk, bi=bi, kv=kv: e.activation(
                        out=PT[:, ps_, kv, bi, :], in_=PS[bk][:], func=AF.Exp, scale=0.125), waits=[tsc[(kv, bi)]])
                    bank_free[bk] = [te]
                    texp.append(te)
                mx[(i, kv)] = texp

        def Vp(i, kv):
            js = [j for j in (i - 1, i, i + 1) if 0 <= j < NT_ALL]
            texp = mx[(i, kv)]
            pb = 0 if kv == 0 else 4
            ps_ = i % 2
            bs = i % 2
            tpv = None
            for bi, j in enumerate(js):
                tpv = P.op("pe", lambda e, j=j, bi=bi: e.matmul(
                    out=PS[pb][:], lhsT=VAUG[:, j % 4, kv, :], rhs=PT[:, ps_, kv, bi, :],
                    start=(bi == 0), stop=False),
                    waits=texp + bank_free[pb] + [t_es, c_tok] if bi == 0 else (), inc=False)
            tpv = P.op("pe", lambda e: e.matmul(out=PS[pb][:], lhsT=ONEROW[0:1, kv, :], rhs=ESROW[0:1, kv, :],
                                                start=False, stop=True), inc=True)
            d0, d1 = (0, 64) if kv == 0 else (64, 128)
            e0, e1 = (64, 128) if kv == 0 else (0, 64)
            rv = RDEN[d0:d1, :].rearrange("p (h q) -> p h q", h=4)
            n1 = P.op("dve", lambda e: e.tensor_copy(out=RDEN[d0:d1, :], in_=PS[pb][e0:e1, :]), waits=[tpv])
            n2 = P.op("act", lambda e: e.activation(out=RDEN[d0:d1, :], in_=RDEN[d0:d1, :], func=AF.Ln), waits=[n1])
            n4 = P.op("act", lambda e: e.activation(out=RDEN[d0:d1, :], in_=RDEN[d0:d1, :], func=AF.Exp, scale=-1.0),
                      waits=[n2])
            mx[(i, "v", kv)] = (n1, n4)

        def Vn(i, kv):
            pb = 0 if kv == 0 else 4
            bs = i % 2
            d0, d1 = (0, 64) if kv == 0 else (64, 128)
            rv = RDEN[d0:d1, :].rearrange("p (h q) -> p h q", h=4)
            n1, n4 = mx[(i, "v", kv)]
            n5 = P.op("dve", lambda e: e.tensor_tensor(out=MIXT[d0:d1, bs, 0:4, :],
                                                       in0=PS[pb][d0:d1, :].rearrange("p (h q) -> p h q", h=4),
                                                       in1=rv, op=ALU.mult), waits=[n4])
            bank_free[pb] = [n1, n5]
            mx[(i, "n", kv)] = n5

        def Pd(i):
            js = [j for j in (i - 1, i, i + 1) if 0 <= j < NT_ALL]
            need = []
            for j in js:
                need += u_done[j]
            tpd = None
            first = True
            for g in range(4):
                for bi, j in enumerate(js):
                    if j < i:
                        v = 0
                    elif j > i:
                        v = 2
                    else:
                        v = 3 if i == 0 else (4 if i == NT_ALL - 1 else 1)
                    tpd = P.op("pe", lambda e, g=g, j=j, v=v, bi=bi: e.matmul(
                        out=PS[6][:, g * 128:(g + 1) * 128], lhsT=U[:, j % 4, 768 + g * 128:768 + (g + 1) * 128],
                        rhs=BPOOL[:, g * 5 + v, :], start=(bi == 0), stop=(bi == len(js) - 1)),
                        waits=need + bank_free[6] + [c_tok] if first else (),
                        inc=(g == 3 and bi == len(js) - 1))
                    first = False
            td_ = P.op("act", lambda e: e.activation(out=DT[:].rearrange("p g t -> p (g t)"), in_=PS[6][:], func=AF.Copy),
                       waits=[tpd])
            bank_free[6] = [td_]
            mx[(i, "d")] = td_

        def Py(i):
            bs = i % 2
            td_ = mx[(i, "d")]
            tpy = None
            for g in range(4):
                tpy = P.op("pe", lambda e, g=g: e.matmul(
                    out=PS[6][:, g * 128:(g + 1) * 128], lhsT=WPOOL[:, g, :], rhs=DT[:, g, :], start=True, stop=True),
                    waits=[td_, late["wB"]] + bank_free[6] if g == 0 else (), inc=(g == 3))
            ty = P.op("dve", lambda e: e.tensor_tensor(out=MIXT[:, bs, 4:8, :],
                                                       in0=PS[6][:].rearrange("p (g t) -> p g t", g=4),
                                                       in1=bc(PSCALE[:], 2, 128), op=ALU.mult), waits=[tpy, c_tok])
            bank_free[6] = [ty]
            mx[(i, "y")] = ty

        def Op(i):
            s = i % HS
            bs = i % 2
            for nh, bk in ((0, 5), (1, 6)):
                w0 = [mx[(i, "n", 0)], mx[(i, "n", 1)], mx[(i, "y")], late["wB"]] + bank_free[bk]
                to = None
                for c in range(8):
                    to = P.op("pe", lambda e, c=c, bk=bk, nh=nh: e.matmul(
                        out=PS[bk][:], lhsT=MIXT[:, bs, c, :], rhs=WOUT[:, c, nh * 512:(nh + 1) * 512],
                        start=(c == 0), stop=(c == 7)), waits=w0 if c == 0 else (), inc=(c == 7))
                te = P.op("dve", lambda e, bk=bk, nh=nh: e.tensor_tensor(
                    out=H[:, s, nh * 512:(nh + 1) * 512], in0=PS[bk][:], in1=H[:, s, nh * 512:(nh + 1) * 512],
                    op=ALU.add), waits=[to, h_ready[i]])
                bank_free[bk] = [te]
                h_ready[i] = te

        RING_N = (XN, "xn", 2, 0)
        RING_C = (XN, "xc", 2, 2)
        RING_A = (XN, "xa", 4, 0)
        xc_free = xn_free

        def stage_b_pro(b_tiles, w_list, na, done):
            pro = [t for t in w_list if t <= b_tiles[0] + 1]
            for t in pro:
                if t in done or t not in h_ready:
                    continue
                done.add(t)
                na[t] = Na(t, RING_N, xn_free)
                tn = Nt(na[t], 1, XNTB[:, t % 2, :, :])
                Wp(t, tn)
                QKT(t)

        def stage_b(p, b_tiles, w_list, na, done):
            xt_c = {}
            ca = {}
            wl = list(w_list)
            stage_b_pro(b_tiles, wl, na, done)
            rest = [t for t in wl if t > b_tiles[0] + 1]
            if rest:
                na[rest[0]] = Na(rest[0], RING_N, xn_free)
            for k in b_tiles:
                sqn = sqc = None
                if (k + 3) in rest:
                    sqn = Na_sq(k + 3, "dve")
                if k - 1 >= b_tiles[0]:
                    sqc = Na_sq(k - 1, "dve")
                Sc2(k)
                Pd(k)
                tn = None
                if (k + 2) in rest:
                    tn = Nt(na[k + 2], 1, XNTB[:, (k + 2) % 2, :, :])
                Vp(k, 0)
                Vp(k, 1)
                Py(k)
                Vn(k, 0)
                Vn(k, 1)
                if sqn is not None:
                    na[k + 3] = Na_rest(k + 3, RING_N, xn_free, sqn, copy_eng="dve")
                if sqc is not None:
                    ca[k - 1] = Na_rest(k - 1, RING_C, xc_free, sqc, copy_eng="dve")
                if (k + 2) in rest:
                    Wp(k + 2, tn)
                Op(k)
                if (k + 2) in rest:
                    QKT(k + 2)
                if k - 1 >= b_tiles[0]:
                    li = k - 1 - b_tiles[0]
                    xt_c[k - 1] = Nt(ca[k - 1], 2, XNT[:, :, li * 128:(li + 1) * 128], bank=2)
            kl = b_tiles[-1]
            ca[kl] = Na(kl, RING_C, xc_free)
            li = kl - b_tiles[0]
            xt_c[kl] = Nt(ca[kl], 2, XNT[:, :, li * 128:(li + 1) * 128])
            return [xt_c[t] for t in b_tiles]

        def final(t):
            s = t % HS
            col = state["st"] * 4
            state["st"] += 1
            t1 = P.op("act", lambda e: e.activation(out=JUNK[:], in_=H[:, s, :], func=AF.Square,
                                                    accum_out=ST[:, col:col + 1]), waits=[h_ready[t], t_init])
            t3 = P.op("act", lambda e: e.activation(out=ST[:, col + 2:col + 3], in_=ST[:, col:col + 1], func=AF.Ln,
                                                    scale=1.0 / D, bias=EPS[:, 0:1]), waits=[t1])
            t4 = P.op("act", lambda e: e.activation(out=ST[:, col + 3:col + 4], in_=ST[:, col + 2:col + 3],
                                                    func=AF.Exp, scale=-0.5), waits=[t3])
            t5 = P.op("dve", lambda e: e.scalar_tensor_tensor(out=H[:, s, :], in0=H[:, s, :],
                                                              scalar=ST[:, col + 3:col + 4], in1=GFIN[:],
                                                              op0=ALU.mult, op1=ALU.mult), waits=[t4, c_tok])
            store_tok[t] = P.dma("pool", out_d[t * 128:(t + 1) * 128, :], H[:, s, :], f"o{s}", waits=[t5])

        a_lists = []
        nxt = 0
        for p in range(NPASS):
            a_hi = min(NT_ALL, (p + 1) * TPP + 1)
            a_lists.append(list(range(nxt, a_hi)))
            nxt = a_hi
        loaded = set()

        def ensure_load(t):
            if t not in loaded and t < NT_ALL:
                loaded.add(t)
                load_x(t)

        xnt_free = None
        na_prev = {}
        pending_fin = []
        for p in range(NPASS):
            b_tiles = list(range(p * TPP, (p + 1) * TPP))
            a_tiles = a_lists[p]
            next_a = a_lists[p + 1] if p + 1 < NPASS else []
            for t in a_tiles:
                ensure_load(t)
            na = dict(na_prev)
            for t in a_tiles[:4]:
                if t not in na:
                    na[t] = Na(t, RING_A, xn_free, sq_eng=("dve" if t % 2 == 0 else "act"))
            xt = []
            if xnt_free is not None:
                P.wait_only("dve", [xnt_free])
            for li, t in enumerate(a_tiles):
                xt.append(Nt(na[t], 0, XNT[:, :, li * 128:(li + 1) * 128], bank=("T" if li % 2 == 0 else 2)))
                if li + 4 < len(a_tiles):
                    t4_ = a_tiles[li + 4]
                    na[t4_] = Na(t4_, RING_A, xn_free, sq_eng=("dve" if t4_ % 2 == 0 else "act"))
                if li == 3 and pending_fin:
                    pending_fin.pop()()
            if pending_fin:
                pending_fin.pop()()
            if p == 0:
                late_sp()
            nb_ = {}
            done_ = set()
            npro = len([t for t in a_tiles if t <= b_tiles[0] + 1])

            sched = {}
            pro_tiles = [t for t in a_tiles if t <= b_tiles[0] + 1]
            tn_ = {}
            for j, t in enumerate(pro_tiles):
                base = len(pro_tiles) + j

                def s_na(t=t, nb_=nb_, done_=done_):
                    done_.add(t)
                    nb_[t] = Na(t, RING_N, xn_free)

                def s_nt(t=t, nb_=nb_, tn_=tn_):
                    tn_[t] = Nt(nb_[t], 1, XNTB[:, t % 2, :, :])

                def s_w(t=t, tn_=tn_):
                    Wp(t, tn_[t])

                def s_q(t=t):
                    QKT(t)
                sched.setdefault(base, []).append(s_na)
                sched.setdefault(base + 1, []).append(s_nt)
                sched.setdefault(base + 2, []).append(s_w)
                sched.setdefault(base + 4, []).append(s_q)
            last_li = len(a_tiles) - 1
            pending = []

            def cb1(li, sched=sched, last_li=last_li):
                for k_ in sorted(sched):
                    if k_ <= li or li == last_li:
                        for f_ in sched.pop(k_):
                            f_()

            xnt_free = ffn(0, a_tiles, xt, on_last_tile=cb1)
            P.wait_only("dve", [xnt_free])
            xt = stage_b(p, b_tiles, a_tiles, nb_, done_)
            if next_a:
                ensure_load(next_a[0])

            def fin(t):
                final(t)
                if (t + HS) in next_a:
                    ensure_load(t + HS)

            na_next = {}

            def cb(li, b_tiles=b_tiles, next_a=next_a, na_next=na_next):
                if li >= 1:
                    fin(b_tiles[li - 1])
                if li >= 4 and li - 4 < len(next_a):
                    tn_ = next_a[li - 4]
                    ensure_load(tn_)
                    na_next[tn_] = Na(tn_, RING_A, xn_free, sq_eng=("dve" if tn_ % 2 == 0 else "act"))

            xnt_free = ffn(1, b_tiles, xt, on_last_tile=cb)
            if p + 1 < NPASS:
                pending_fin.append(lambda fin=fin, t=b_tiles[-1]: fin(t))
            else:
                fin(b_tiles[-1])
            na_prev = na_next
        P.wait_only("sp", [store_tok[t] for t in range(NT_ALL)])

        block = E(nc.Block())

        @block.tensor
        def _(e):
            P.emit("pe", e, sems)

        @block.scalar
        def _(e):
            P.emit("act", e, sems)

        @block.vector
        def _(e):
            P.emit("dve", e, sems)

        @block.gpsimd
        def _(e):
            P.emit("pool", e, sems)

        @block.sync
        def _(e):
            P.emit("sp", e, sems)
    return nc


def _pool_consts(S=S):
    NT_ALL = S // 128
    wins = (2, 4, 8, 16)
    B = np.zeros((128, 20, 128), np.float64)
    for g, w in enumerate(wins):
        half = w // 2
        for (tile_i, variants) in ((0, {0: 3, 1: 2}), (1, {0: 0, 1: 1, 2: 2}), (NT_ALL - 1, {NT_ALL - 2: 0, NT_ALL - 1: 4})):
            for tl in range(128):
                t = tile_i * 128 + tl
                coef = {}
                for (lo, hi) in ((t - half, t + half - 1), (t - half + 1, t + half)):
                    a = min(max(lo, 0), S)
                    b = min(max(hi + 1, 0), S)
                    for tp in range(a, b):
                        coef[tp] = coef.get(tp, 0.0) + 0.5 / (b - a)
                coef[t] = coef.get(t, 0.0) - 1.0
                for tp, cval in coef.items():
                    j = tp // 128
                    if j not in variants:
                        continue
                    B[tp % 128, g * 5 + variants[j], tl] = cval
    return B.astype(np.float32)


def _consts(S=S):
    NT_ALL = S // 128
    ident = np.eye(128, dtype=np.float32)
    b = np.arange(128)[:, None]
    a = np.arange(128)[None, :]
    m0 = np.where(a > b, NEG, 0.0).astype(np.float32)
    m1 = np.where(b > a, NEG, 0.0).astype(np.float32)
    negm = np.stack([np.tile(m0, (1, 4)), np.tile(m1, (1, 4))], axis=1)
    inv_freq = np.float32(500000.0) ** (-(np.arange(0, 16, 2, dtype=np.float32) / np.float32(16)))
    ang = np.arange(S, dtype=np.float32)[:, None] * inv_freq[None, :].astype(np.float32)
    emb = np.concatenate([ang, ang], axis=-1).astype(np.float32)
    cos = np.cos(emb.astype(np.float64)).astype(np.float32)
    sin = np.sin(emb.astype(np.float64)).astype(np.float32)
    sin_s = sin.copy()
    sin_s[:, 0:8] = -sin[:, 0:8]
    cosT = np.ascontiguousarray(cos.reshape(NT_ALL, 128, 16).transpose(1, 0, 2))
    sinT = np.ascontiguousarray(sin_s.reshape(NT_ALL, 128, 16).transpose(1, 0, 2))
    return ident, np.ascontiguousarray(negm), _pool_consts(S), cosT, sinT


_CACHE = {}


def prep_shared(ffn1_norm, ffn1_w_gate, ffn1_w_up, ffn1_w_down, mix_norm, w_in,
                sink_logits, pool_w, pool_scale, w_out, ffn2_norm, ffn2_w_gate,
                ffn2_w_up, ffn2_w_down, final_norm, S_=S):
    f = lambda a: np.ascontiguousarray(np.asarray(a, dtype=np.float32))
    ident, negm, bpool, cosT, sinT = _consts(S_)

    def gu_layout(wg, wu):
        arr = np.stack([f(wg)[0], f(wu)[0]])
        arr = arr.reshape(2, 8, 128, NCH, 128)
        arr = arr.transpose(3, 2, 0, 1, 4)
        return np.ascontiguousarray(arr).reshape(NCH, 128, 2 * 8 * 128)

    perm_q = []
    for c in range(4):
        perm_q += list(range(c * 64, (c + 1) * 64)) + list(range((4 + c) * 64, (5 + c) * 64))
    perm_q = np.array(perm_q)
    col_perm = np.concatenate([perm_q, np.arange(512, 1280)])
    row_perm = np.concatenate([perm_q, np.arange(512, 1024)])
    win_p = np.ascontiguousarray(f(w_in)[0][:, col_perm])
    wout_p = np.ascontiguousarray(f(w_out)[0][row_perm, :])
    wpool = np.ascontiguousarray(f(pool_w)[0].transpose(1, 0, 2))
    gs = np.stack([f(ffn1_norm)[0], f(mix_norm)[0], f(ffn2_norm)[0]])
    gfm = np.ascontiguousarray(gs.reshape(3, 8, 128).transpose(2, 0, 1))
    gfin = np.ascontiguousarray(np.broadcast_to(f(final_norm)[None, :], (128, D)))
    pscale = np.ascontiguousarray(f(pool_scale)[0].reshape(4, 128).T)
    sk = f(sink_logits)[0]
    sinkb = np.ascontiguousarray(np.concatenate([np.broadcast_to(sk[0:4][None, :], (64, 4)),
                                                 np.broadcast_to(sk[4:8][None, :], (64, 4))], axis=0))
    return {
        "wgu1": gu_layout(ffn1_w_gate, ffn1_w_up), "wgu2": gu_layout(ffn2_w_gate, ffn2_w_up),
        "wd1": f(ffn1_w_down)[0], "wd2": f(ffn2_w_down)[0],
        "win": win_p, "wout": wout_p, "wpool": wpool, "ident": ident, "negm": negm, "bpool": bpool,
        "cosT": cosT, "sinT": sinT, "gfm": gfm, "gfin": gfin, "pscale": pscale, "sinkb": sinkb,
    }


def kernel(x, **params):
    x = np.ascontiguousarray(np.asarray(x, dtype=np.float32))
    if "nc" not in _CACHE:
        _CACHE["nc"] = build_program()
    nc = _CACHE["nc"]
    shared = prep_shared(**params)
    in_maps = []
    for c in range(8):
        m = dict(shared)
        m["x"] = x[c]
        in_maps.append(m)
    res = run_bass_kernel_spmd(nc, in_maps, core_ids=list(range(8)))
    return np.stack([np.asarray(r["out"], dtype=np.float32) for r in res.results], axis=0)
```

```python
import numpy as np
from contextlib import ExitStack

import concourse.bass as bass
import concourse.mybir as mybir
from concourse.bass_utils import run_bass_kernel_spmd

F32 = mybir.dt.float32
BF16 = mybir.dt.bfloat16
AF = mybir.ActivationFunctionType
ALU = mybir.AluOpType

S = 4096
D = 1024
DFF = 2816
NT_ALL = 32
NCH = 22
BLOCKS = [list(range(0, 6)), list(range(6, 12)), list(range(12, 17)), list(range(17, 22))]
NPASS = 4
TPP = 8
HS = 10
NEG = -30000.0


def bc(ap, axis, n):
    shp = list(ap.shape)
    a = ap.unsqueeze(axis)
    shp.insert(axis, n)
    return a.broadcast_to(shp)


class Plan:
    ENGS = ("pe", "act", "dve", "pool", "sp")

    def __init__(self):
        self.ops = {e: [] for e in self.ENGS}
        self.cnt = {}
        self.seen = {e: {} for e in self.ENGS}

    def _filter(self, eng, waits):
        w = []
        for t in waits:
            if t is None:
                continue
            if isinstance(t, list):
                for tt in t:
                    waits.append(tt)
                continue
            key, val = t
            if self.seen[eng].get(key, 0) >= val:
                continue
            self.seen[eng][key] = val
            w = [(k_, v_) for (k_, v_) in w if k_ != key]
            w.append((key, val))
        return w

    def op(self, eng, fn, waits=(), inc=True):
        w = self._filter(eng, list(waits))
        tok = None
        incinfo = None
        if inc:
            self.cnt[eng] = self.cnt.get(eng, 0) + 1
            tok = (eng, self.cnt[eng])
            incinfo = (eng, 1)
        self.ops[eng].append((w, fn, incinfo))
        return tok

    def dma(self, eng, out, in_, key, waits=()):
        w = self._filter(eng, list(waits))
        self.cnt[key] = self.cnt.get(key, 0) + 16
        tok = (key, self.cnt[key])
        self.ops[eng].append((w, lambda e, o=out, i=in_: e.dma_start(out=o, in_=i), (key, 16)))
        return tok

    def wait_only(self, eng, waits):
        w = self._filter(eng, list(waits))
        if w:
            self.ops[eng].append((w, None, None))

    def emit(self, eng, e, sems):
        for (w, fn, incinfo) in self.ops[eng]:
            if fn is None:
                for (key, val) in w:
                    e.wait_ge(sems[key], val)
                continue
            for (key, val) in w[:-1]:
                e.wait_ge(sems[key], val)
            ins = fn(e)
            if w:
                ins._wait_ge(sems[w[-1][0]], w[-1][1])
            if incinfo is not None:
                ins.then_inc(sems[incinfo[0]], incinfo[1])


def build_program(S=S):
    NT_ALL = S // 128
    NPASS = NT_ALL // TPP
    nc = bass.Bass("TRN2", target_bir_lowering=False)
    dt = nc.dram_tensor
    x_d = dt("x", [S, D], F32, kind="ExternalInput").ap()
    wgu_d = [dt("wgu1", [NCH, 128, 2 * 8 * 128], F32, kind="ExternalInput").ap(),
             dt("wgu2", [NCH, 128, 2 * 8 * 128], F32, kind="ExternalInput").ap()]
    wd_d = [dt("wd1", [DFF, D], F32, kind="ExternalInput").ap(),
            dt("wd2", [DFF, D], F32, kind="ExternalInput").ap()]
    win_d = dt("win", [D, 1280], F32, kind="ExternalInput").ap()
    wout_d = dt("wout", [D, D], F32, kind="ExternalInput").ap()
    wpool_d = dt("wpool", [128, 4, 128], F32, kind="ExternalInput").ap()
    ident_d = dt("ident", [128, 128], F32, kind="ExternalInput").ap()
    negm_d = dt("negm", [128, 2, 512], F32, kind="ExternalInput").ap()
    bpool_d = dt("bpool", [128, 20, 128], F32, kind="ExternalInput").ap()
    cos_d = dt("cosT", [128, NT_ALL, 16], F32, kind="ExternalInput").ap()
    sin_d = dt("sinT", [128, NT_ALL, 16], F32, kind="ExternalInput").ap()
    gfm_d = dt("gfm", [128, 3, 8], F32, kind="ExternalInput").ap()
    gfin_d = dt("gfin", [128, D], F32, kind="ExternalInput").ap()
    pscale_d = dt("pscale", [128, 4], F32, kind="ExternalInput").ap()
    sink_d = dt("sinkb", [128, 4], F32, kind="ExternalInput").ap()
    out_d = dt("out", [S, D], F32, kind="ExternalOutput").ap()

    with ExitStack() as es:
        E = es.enter_context
        sb = lambda name, shape, dty: E(nc.sbuf_tensor(name, shape, dty))
        H = sb("H", [128, HS, D], F32)
        XNT = sb("XNT", [128, 8, 9 * 128], BF16)
        HID = sb("HID", [128, 6, 9 * 128], BF16)
        WD = sb("WD", [128, 6, D], BF16)
        WGU = sb("WGU", [128, 4, 2 * 8 * 128], BF16)
        SG = sb("SG", [128, 2, 512], BF16)
        JUNK = sb("JUNK", [128, D], BF16)
        JUNK2 = sb("JUNK2", [128, D], BF16)
        XN = sb("XN", [128, 4, D], BF16)
        EPS = sb("EPS", [128, 1], F32)
        ST = sb("ST", [128, 4 * 4 * NT_ALL + 64], F32)
        WIN = sb("WIN", [128, 8, 1280], BF16)
        WOUT = sb("WOUT", [128, 8, D], BF16)
        WPOOL = sb("WPOOL", [128, 4, 128], BF16)
        IDB = sb("IDB", [128, 128], BF16)
        NEGM = sb("NEGM", [128, 2, 512], BF16)
        BPOOL = sb("BPOOL", [128, 20, 128], BF16)
        COS = sb("COS", [128, NT_ALL, 16], F32)
        SIN = sb("SIN", [128, NT_ALL, 16], F32)
        GFM = sb("GFM", [128, 3, 8], F32)
        GFIN = sb("GFIN", [128, D], F32)
        PSCALE = sb("PSCALE", [128, 4], F32)
        SINKB = sb("SINKB", [128, 4], F32)
        ESINK = sb("ESINK", [128, 4], F32)
        ESROW = sb("ESROW", [1, 2, 512], BF16)
        ONEROW = sb("ONEROW", [1, 2, 128], BF16)
        U = sb("U", [128, 4, 1280], BF16)
        VAUG = sb("VAUG", [128, 4, 2, 128], BF16)
        QT = sb("QT", [128, 3, 4, 128], BF16)
        KT = sb("KT", [128, 4, 128], BF16)
        XNTB = sb("XNTB", [128, 2, 8, 128], BF16)
        MIXT = sb("MIXT", [128, 2, 8, 128], BF16)
        RDEN = sb("RDEN", [128, 512], F32)
        DT = sb("DT", [128, 4, 128], BF16)
        KRAW = sb("KRAW", [128, 2, 128], F32)
        QRAW = sb("QRAW", [128, 2, 512], F32)
        RT = sb("RT", [128, 2, 10, 16], F32)
        RT2 = sb("RT2", [128, 2, 10, 16], F32)
        PS = [E(nc.psum_tensor(f"ps{i}", [128, 512], F32)) for i in range(7)]
        PST = E(nc.psum_tensor("pst", [128, 8, 128], BF16))
        PSTF = PST[:].rearrange("p a b -> p (a b)").bitcast(F32)
        PT = HID[:].rearrange("p a b -> p (a b)")[:, 0:6144].rearrange("p (s k b n) -> p s k b n", s=2, k=2, b=3)

        P = Plan()
        sem_keys = ["pe", "act", "dve", "pool", "c", "cg", "c2", "cg2", "wd", "wB"] + [f"x{i}" for i in range(HS)] + \
                   [f"o{i}" for i in range(HS)] + [f"wgu{i}" for i in range(4)]
        sems = {k: E(nc.semaphore(k)) for k in sem_keys}

        bank_free = {i: [] for i in range(7)}
        bank_free["T"] = []
        h_ready = {}
        store_tok = {}
        state = {"gu": 0, "pair": 0, "sgs": 0, "dn": 0, "xn": 0, "xc": 0, "xa": 0, "st": 0}
        gu_free = [None] * 4
        sg_free = [None] * 2
        xn_free = [None] * 4
        wd_free = [None]

        P.dma("sp", GFM[:], gfm_d, "c")
        P.dma("sp", PSCALE[:], pscale_d, "c")
        c_sp = P.dma("sp", SINKB[:], sink_d, "c")
        tok_idb = P.dma("pool", IDB[:], ident_d, "cg")
        c_tok = [c_sp, tok_idb]
        late = {}

        def late_sp():
            P.dma("sp", COS[:], cos_d, "c2")
            P.dma("sp", SIN[:], sin_d, "c2")
            c_tok.append(P.dma("sp", GFIN[:], gfin_d, "c2"))

        def late_pool(b):
            if b == 1:
                P.dma("pool", WIN[:], win_d.rearrange("(k p) n -> p k n", p=128), "wB")
                late["b1"] = True
            elif b == 2:
                P.dma("pool", WOUT[:], wout_d.rearrange("(k p) n -> p k n", p=128), "wB")
                late["b2"] = True
            elif b == 3:
                P.dma("pool", NEGM[:], negm_d, "cg2")
                c_tok.append(P.dma("pool", BPOOL[:], bpool_d, "cg2"))
                late["wB"] = P.dma("pool", WPOOL[:], wpool_d, "wB")

        t_init = P.op("dve", lambda e: e.memset(ST[:], 0.0))
        t_init = P.op("dve", lambda e: e.memset(EPS[:], 1e-6))
        t_init = P.op("dve", lambda e: e.memset(VAUG[:], 1.0))
        t_es = P.op("act", lambda e: e.activation(out=ESINK[:], in_=SINKB[:], func=AF.Exp), waits=[c_tok])
        t_es1 = P.op("act", lambda e: e.activation(out=ESROW[0:1, 0, :].rearrange("p (h q) -> p h q", h=4),
                                                   in_=bc(ESINK[0:1, :], 2, 128), func=AF.Copy), waits=[t_es])
        t_es = P.op("act", lambda e: e.activation(out=ESROW[0:1, 1, :].rearrange("p (h q) -> p h q", h=4),
                                                  in_=bc(ESINK[64:65, :], 2, 128), func=AF.Copy), waits=[t_es])
        P.op("dve", lambda e: e.memset(ONEROW[0:1, 0, 0:64], 0.0))
        P.op("dve", lambda e: e.memset(ONEROW[0:1, 0, 64:128], 1.0))
        P.op("dve", lambda e: e.memset(ONEROW[0:1, 1, 0:64], 1.0))
        t_one = P.op("dve", lambda e: e.memset(ONEROW[0:1, 1, 64:128], 0.0))
        t_es = [t_es, t_one]

        def load_x(t):
            s = t % HS
            w = [store_tok.get(t - HS)]
            h_ready[t] = P.dma("sp", H[:, s, :], x_d[t * 128:(t + 1) * 128, :], f"x{s}", waits=w)

        def norm_T(t, which, dstT):
            s = t % HS
            col = state["st"] * 4
            state["st"] += 1
            t1 = P.op("act", lambda e: e.activation(out=JUNK[:], in_=H[:, s, :], func=AF.Square,
                                                    accum_out=ST[:, col:col + 1]),
                      waits=[h_ready[t], t_init])
            t2 = P.op("dve", lambda e: e.tensor_scalar(out=ST[:, col + 1:col + 2], in0=ST[:, col:col + 1],
                                                       scalar1=1.0 / D, scalar2=1e-6, op0=ALU.mult, op1=ALU.add),
                      waits=[t1])
            t3 = P.op("act", lambda e: e.activation(out=ST[:, col + 2:col + 3], in_=ST[:, col + 1:col + 2], func=AF.Ln),
                      waits=[t2])
            t4 = P.op("act", lambda e: e.activation(out=ST[:, col + 3:col + 4], in_=ST[:, col + 2:col + 3],
                                                    func=AF.Exp, scale=-0.5), waits=[t3])
            xs = state["xn"] % 2
            state["xn"] += 1
            t5 = P.op("act", lambda e: e.activation(out=XN[:, xs, :], in_=H[:, s, :], func=AF.Copy,
                                                    scale=ST[:, col + 3:col + 4]), waits=[t4, xn_free[xs]])
            t6 = None
            for k in range(8):
                t6 = P.op("pe", lambda e, k=k: e.transpose(out=PST[:, k, :], in_=XN[:, xs, k * 128:(k + 1) * 128],
                                                           identity=IDB[:]),
                          waits=[t5, c_tok] + bank_free["T"] if k == 0 else (), inc=(k == 7))
            xn_free[xs] = t6
            t7 = P.op("dve", lambda e: e.tensor_tensor(out=dstT, in0=PST[:], in1=bc(GFM[:, which, :], 2, 128),
                                                       op=ALU.mult), waits=[t6, c_tok])
            bank_free["T"] = [t7]
            return t7, col + 3

        gu_seq = [(n_, c_) for _p in range(NPASS) for n_ in (0, 1) for c_ in range(NCH)]
        gu_tok = {}
        gu_done = {}

        def plan_gu_load(g):
            if g >= len(gu_seq):
                return
            n_, c_ = gu_seq[g]
            slot = g % 4
            gu_tok[g] = P.dma("pool", WGU[:, slot, :], wgu_d[n_][c_], f"wgu{slot}", waits=[gu_done.get(g - 4)])

        for g0 in range(4):
            plan_gu_load(g0)

        def ffn(n, tiles, xnt_toks, on_last_tile=None):
            NTl = len(tiles)
            groups = [(o, min(512, NTl * 128 - o)) for o in range(0, NTl * 128, 512)]
            last_gate_up = None
            for b, chunks in enumerate(BLOCKS):
                nb = len(chunks)

                def plan_wd(chunks=chunks, nb=nb):
                    return P.dma("pool", WD[:, 0:nb, :],
                                 wd_d[n][chunks[0] * 128:(chunks[-1] + 1) * 128, :].rearrange("(c p) n -> p c n", p=128),
                                 "wd", waits=[wd_free[0]])
                tok_wd = None
                if b != 0:
                    tok_wd = plan_wd()
                if "wB" not in late and b >= 1:
                    late_pool(b)
                th = None
                for ci, c in enumerate(chunks):
                    g = state["gu"]
                    slot = g % 4
                    state["gu"] += 1
                    assert gu_seq[g] == (n, c)
                    tok_gu = gu_tok[g]
                    wv = WGU[:, slot, :].rearrange("p (g k j) -> p g k j", g=2, k=8)
                    tu = None
                    for (o, nn) in groups:
                        pr = state["pair"] % 2
                        state["pair"] += 1
                        bg, bu = 2 * pr, 2 * pr + 1
                        lastli = (o + nn) // 128 - 1
                        w0 = [tok_gu, xnt_toks[lastli]] + bank_free[bg] + bank_free[bu]
                        tg = None
                        for k in range(8):
                            tg = P.op("pe", lambda e, k=k, bg=bg, o=o, nn=nn, wv=wv: e.matmul(
                                out=PS[bg][:, 0:nn], lhsT=wv[:, 0, k, :], rhs=XNT[:, k, o:o + nn],
                                start=(k == 0), stop=(k == 7)), waits=w0 if k == 0 else (), inc=(k == 7))
                        for k in range(8):
                            tu = P.op("pe", lambda e, k=k, bu=bu, o=o, nn=nn, wv=wv: e.matmul(
                                out=PS[bu][:, 0:nn], lhsT=wv[:, 1, k, :], rhs=XNT[:, k, o:o + nn],
                                start=(k == 0), stop=(k == 7)), inc=(k == 7))
                        ss = state["sgs"] % 2
                        state["sgs"] += 1
                        ts = P.op("act", lambda e, bg=bg, nn=nn, ss=ss: e.activation(
                            out=SG[:, ss, 0:nn], in_=PS[bg][:, 0:nn], func=AF.Silu), waits=[tg, sg_free[ss]])
                        th = P.op("dve", lambda e, bu=bu, nn=nn, ss=ss, ci=ci, o=o: e.tensor_tensor(
                            out=HID[:, ci, o:o + nn], in0=PS[bu][:, 0:nn], in1=SG[:, ss, 0:nn], op=ALU.mult),
                            waits=[tu, ts])
                        bank_free[bg] = [th]
                        bank_free[bu] = [th]
                        sg_free[ss] = th
                    gu_done[g] = tu
                    plan_gu_load(g + 4)
                    if b == 0 and ci == 1:
                        tok_wd = plan_wd()
                    last_gate_up = tu
                td = None
                four = not (on_last_tile is not None and b == len(BLOCKS) - 1 and n == 0)
                for li, t in enumerate(tiles):
                    s = t % HS
                    if four:
                        pairs = [(4, PS[4][:]), (5, PS[5][:])] if li % 2 == 0 else [(6, PS[6][:]), ("T", PSTF)]
                    else:
                        pairs = None
                    for nh in range(2):
                        if four:
                            bd, bap = pairs[nh]
                            w0 = ([tok_wd, th] + bank_free[pairs[0][0]] + bank_free[pairs[1][0]]) if nh == 0 else []
                        else:
                            bd = 4 + (state["dn"] % 2)
                            bap = PS[bd][:]
                            state["dn"] += 1
                            w0 = [tok_wd, th] + bank_free[bd]
                        for ci in range(nb):
                            td = P.op("pe", lambda e, ci=ci, bap=bap, li=li, nh=nh, nb=nb: e.matmul(
                                out=bap, lhsT=HID[:, ci, li * 128:(li + 1) * 128],
                                rhs=WD[:, ci, nh * 512:(nh + 1) * 512], start=(ci == 0), stop=(ci == nb - 1)),
                                waits=w0 if ci == 0 else (), inc=(ci == nb - 1))
                        te = P.op("dve", lambda e, bap=bap, s=s, nh=nh: e.scalar_tensor_tensor(
                            out=H[:, s, nh * 512:(nh + 1) * 512], in0=bap, scalar=0.5,
                            in1=H[:, s, nh * 512:(nh + 1) * 512], op0=ALU.mult, op1=ALU.add),
                            waits=[td, h_ready[t]])
                        bank_free[bd] = [te]
                        h_ready[t] = te
                    if on_last_tile is not None and b == len(BLOCKS) - 1:
                        on_last_tile(li)
                wd_free[0] = td
            return last_gate_up

        u_done = {}
        w_tok = {}

        def Na_sq(t, sq_eng="act"):
            s = t % HS
            col = state["st"] * 4
            state["st"] += 1
            if sq_eng == "act":
                t1 = P.op("act", lambda e: e.activation(out=JUNK[:], in_=H[:, s, :], func=AF.Square,
                                                        accum_out=ST[:, col:col + 1]),
                          waits=[h_ready[t], t_init])
            else:
                t1 = P.op("dve", lambda e: e.scalar_tensor_tensor(out=JUNK2[:], in0=H[:, s, :], scalar=1.0,
                                                                  in1=H[:, s, :], op0=ALU.mult, op1=ALU.mult,
                                                                  accum_out=ST[:, col:col + 1]),
                          waits=[h_ready[t], t_init, state.get("junk2")])
                state["junk2"] = t1
            return (col, t1)

        def Na_rest(t, ring, free_list, sq, copy_eng="act"):
            s = t % HS
            col, t1 = sq
            xs = ring[3] + state[ring[1]] % ring[2]
            state[ring[1]] += 1
            buf = ring[0]
            t3 = P.op("act", lambda e: e.activation(out=ST[:, col + 2:col + 3], in_=ST[:, col:col + 1], func=AF.Ln,
                                                    scale=1.0 / D, bias=EPS[:, 0:1]), waits=[t1])
            t4 = P.op("act", lambda e: e.activation(out=ST[:, col + 3:col + 4], in_=ST[:, col + 2:col + 3],
                                                    func=AF.Exp, scale=-0.5), waits=[t3])
            if copy_eng == "act":
                t5 = P.op("act", lambda e: e.activation(out=buf[:, xs, :], in_=H[:, s, :], func=AF.Copy,
                                                        scale=ST[:, col + 3:col + 4]), waits=[t4, free_list[xs]])
            else:
                t5 = P.op(copy_eng, lambda e: e.tensor_scalar(out=buf[:, xs, :], in0=H[:, s, :],
                                                            scalar1=ST[:, col + 3:col + 4], scalar2=None,
                                                            op0=ALU.mult), waits=[t4, free_list[xs]])
            return (t5, xs, buf, free_list)

        def Na(t, ring, free_list, sq_eng="act"):
            return Na_rest(t, ring, free_list, Na_sq(t, sq_eng))

        PS2B = PS[2][:].bitcast(BF16).rearrange("p (k t) -> p k t", k=8)

        def Nt(na, which, dstT, bank="T"):
            t5, xs, buf, free_list = na
            pt = PST if bank == "T" else PS2B
            t6 = None
            for k in range(8):
                t6 = P.op("pe", lambda e, k=k: e.transpose(out=pt[:, k, :], in_=buf[:, xs, k * 128:(k + 1) * 128],
                                                           identity=IDB[:]),
                          waits=[t5, c_tok] + bank_free[bank] if k == 0 else (), inc=(k == 7))
            free_list[xs] = t6
            t7 = P.op("dve", lambda e: e.tensor_tensor(out=dstT, in0=pt[:], in1=bc(GFM[:, which, :], 2, 128),
                                                       op=ALU.mult), waits=[t6, c_tok])
            bank_free[bank] = [t7]
            return t7

        def Wp(t, tn):
            us = t % 4
            bs = t % 2
            qs = t % 3
            segs = [(0, 512, 1), (512, 512, 2), (1024, 256, 3)]
            toks = []
            for (co, cn, bk) in segs:
                w0 = [tn, late["wB"]] + bank_free[bk]
                tk = None
                for k in range(8):
                    tk = P.op("pe", lambda e, k=k, co=co, cn=cn, bk=bk: e.matmul(
                        out=PS[bk][:, 0:cn], lhsT=XNTB[:, bs, k, :], rhs=WIN[:, k, co:co + cn],
                        start=(k == 0), stop=(k == 7)), waits=w0 if k == 0 else (), inc=(k == 7))
                toks.append(tk)
            qv = QRAW[:, bs, :].rearrange("p (h d) -> p h d", d=64)
            kv_ = KRAW[:, bs, :].rearrange("p (h d) -> p h d", d=64)
            uq = U[:, us, 0:512].rearrange("p (h d) -> p h d", d=64)
            uk = U[:, us, 512:640].rearrange("p (h d) -> p h d", d=64)
            cosv = COS[:, t, :]
            sinv = SIN[:, t, :]
            tb0 = P.op("act", lambda e: e.activation(out=KRAW[:, bs, :], in_=PS[2][:, 0:128], func=AF.Copy),
                       waits=[toks[1]])
            ta0 = P.op("act", lambda e: e.activation(out=QRAW[:, bs, :], in_=PS[1][:, 0:512], func=AF.Copy),
                       waits=[toks[0], c_tok])
            ta = P.op("act", lambda e: e.activation(out=U[:, us, 0:512], in_=PS[1][:, 0:512], func=AF.Copy))
            tb = P.op("act", lambda e: e.activation(out=U[:, us, 512:640], in_=PS[2][:, 0:128], func=AF.Copy))
            last_rope = None
            for (src, dst, nh_, hoff) in ((qv, uq, 8, 0), (kv_, uk, 2, 8)):
                r1 = RT[:, bs, hoff:hoff + nh_, :]
                r2 = RT2[:, bs, hoff:hoff + nh_, :]
                a1 = P.op("dve", lambda e, src=src, r1=r1, nh_=nh_: e.tensor_tensor(
                    out=r1, in0=src[:, :, 0:16], in1=bc(cosv, 1, nh_), op=ALU.mult), waits=[ta0, tb0])
                a2 = P.op("dve", lambda e, src=src, r2=r2, nh_=nh_: e.tensor_tensor(
                    out=r2[:, :, 0:8], in0=src[:, :, 8:16], in1=bc(sinv[:, 0:8], 1, nh_), op=ALU.mult))
                a3 = P.op("dve", lambda e, src=src, r2=r2, nh_=nh_: e.tensor_tensor(
                    out=r2[:, :, 8:16], in0=src[:, :, 0:8], in1=bc(sinv[:, 8:16], 1, nh_), op=ALU.mult))
                a4 = P.op("dve", lambda e, dst=dst, r1=r1, r2=r2: e.tensor_tensor(
                    out=dst[:, :, 0:16], in0=r1, in1=r2, op=ALU.add), waits=[a1, a2, a3, ta, tb])
                last_rope = a4
            tv0 = P.op("act", lambda e: e.activation(out=VAUG[:, us, 0, 0:64], in_=PS[2][:, 128:192], func=AF.Copy),
                       waits=[toks[1], t_init])
            tv1 = P.op("act", lambda e: e.activation(out=VAUG[:, us, 1, 64:128], in_=PS[2][:, 192:256], func=AF.Copy))
            tp0 = P.op("act", lambda e: e.activation(out=U[:, us, 768:1024], in_=PS[2][:, 256:512], func=AF.Copy))
            tp1 = P.op("act", lambda e: e.activation(out=U[:, us, 1024:1280], in_=PS[3][:, 0:256], func=AF.Copy),
                       waits=[toks[2]])
            bank_free[1] = [ta]
            bank_free[2] = [tb, tp0]
            bank_free[3] = [tp1]
            w_tok[t] = (last_rope, [tv1, tp1, tp0])

        def QKT(t):
            us = t % 4
            qs = t % 3
            last_rope, others = w_tok[t]
            tt = None
            for c5 in range(5):
                tt = P.op("pe", lambda e, c5=c5: e.transpose(out=PST[:, c5, :], in_=U[:, us, c5 * 128:(c5 + 1) * 128],
                                                             identity=IDB[:]),
                          waits=[last_rope] + bank_free["T"] if c5 == 0 else (), inc=(c5 == 4))
            tq = P.op("act", lambda e: e.activation(out=QT[:, qs, :, :], in_=PST[:, 0:4, :], func=AF.Copy), waits=[tt])
            tk_ = P.op("act", lambda e: e.activation(out=KT[:, us, :], in_=PST[:, 4, :], func=AF.Copy))
            bank_free["T"] = [tq, tk_]
            u_done[t] = [tq, tk_, last_rope] + others

        mx = {}

        def Sc2(i):
            js = [j for j in (i - 1, i, i + 1) if 0 <= j < NT_ALL]
            need = []
            for j in js:
                need += u_done[j]
            ps_ = i % 2
            qs = i % 3
            sb_ = {0: [0, 1, 2], 1: [3, 4, 5]}
            tsc = {}
            first = True
            for bi, j in enumerate(js):
                for kv in range(2):
                    r0, r1 = kv * 64, (kv + 1) * 64
                    bk = sb_[kv][bi]
                    diag = (j == i)
                    w0 = (need + [c_tok] if first else []) + bank_free[bk]
                    first = False
                    tsc[(kv, bi)] = P.op("pe", lambda e, bk=bk, j=j, r0=r0, r1=r1, diag=diag: e.matmul(
                        out=PS[bk][:], lhsT=KT[r0:r1, j % 4, :], rhs=QT[r0:r1, qs, :, :],
                        start=True, stop=diag), waits=w0, inc=diag)
            for bi, j in enumerate(js):
                if j == i:
                    continue
                which = 0 if j < i else 1
                for kv in range(2):
                    bk = sb_[kv][bi]
                    tsc[(kv, bi)] = P.op("pe", lambda e, bk=bk, which=which: e.matmul(
                        out=PS[bk][:], lhsT=IDB[:], rhs=NEGM[:, which, :], start=False, stop=True), inc=True)
            for kv in range(2):
                texp = []
                for bi, j in enumerate(js):
                    bk = sb_[kv][bi]
                    te = P.op("act", lambda e, bk=bk, bi=bi, kv=kv: e.activation(
                        out=PT[:, ps_, kv, bi, :], in_=PS[bk][:], func=AF.Exp, scale=0.125), waits=[tsc[(kv, bi)]])
                    bank_free[bk] = [te]
                    texp.append(te)
                mx[(i, kv)] = texp

        def Vp(i, kv):
            js = [j for j in (i - 1, i, i + 1) if 0 <= j < NT_ALL]
            texp = mx[(i, kv)]
            pb = 0 if kv == 0 else 4
            ps_ = i % 2
            bs = i % 2
            tpv = None
            for bi, j in enumerate(js):
                tpv = P.op("pe", lambda e, j=j, bi=bi: e.matmul(
                    out=PS[pb][:], lhsT=VAUG[:, j % 4, kv, :], rhs=PT[:, ps_, kv, bi, :],
                    start=(bi == 0), stop=False),
                    waits=texp + bank_free[pb] + [t_es, c_tok] if bi == 0 else (), inc=False)
            tpv = P.op("pe", lambda e: e.matmul(out=PS[pb][:], lhsT=ONEROW[0:1, kv, :], rhs=ESROW[0:1, kv, :],
                                                start=False, stop=True), inc=True)
            d0, d1 = (0, 64) if kv == 0 else (64, 128)
            e0, e1 = (64, 128) if kv == 0 else (0, 64)
            rv = RDEN[d0:d1, :].rearrange("p (h q) -> p h q", h=4)
            n1 = P.op("dve", lambda e: e.tensor_copy(out=RDEN[d0:d1, :], in_=PS[pb][e0:e1, :]), waits=[tpv])
            n2 = P.op("act", lambda e: e.activation(out=RDEN[d0:d1, :], in_=RDEN[d0:d1, :], func=AF.Ln), waits=[n1])
            n4 = P.op("act", lambda e: e.activation(out=RDEN[d0:d1, :], in_=RDEN[d0:d1, :], func=AF.Exp, scale=-1.0),
                      waits=[n2])
            mx[(i, "v", kv)] = (n1, n4)

        def Vn(i, kv):
            pb = 0 if kv == 0 else 4
            bs = i % 2
            d0, d1 = (0, 64) if kv == 0 else (64, 128)
            rv = RDEN[d0:d1, :].rearrange("p (h q) -> p h q", h=4)
            n1, n4 = mx[(i, "v", kv)]
            n5 = P.op("dve", lambda e: e.tensor_tensor(out=MIXT[d0:d1, bs, 0:4, :],
                                                       in0=PS[pb][d0:d1, :].rearrange("p (h q) -> p h q", h=4),
                                                       in1=rv, op=ALU.mult), waits=[n4])
            bank_free[pb] = [n1, n5]
            mx[(i, "n", kv)] = n5

        def Pd(i):
            js = [j for j in (i - 1, i, i + 1) if 0 <= j < NT_ALL]
            need = []
            for j in js:
                need += u_done[j]
            tpd = None
            first = True
            for g in range(4):
                for bi, j in enumerate(js):
                    if j < i:
                        v = 0
                    elif j > i:
                        v = 2
                    else:
                        v = 3 if i == 0 else (4 if i == NT_ALL - 1 else 1)
                    tpd = P.op("pe", lambda e, g=g, j=j, v=v, bi=bi: e.matmul(
                        out=PS[6][:, g * 128:(g + 1) * 128], lhsT=U[:, j % 4, 768 + g * 128:768 + (g + 1) * 128],
                        rhs=BPOOL[:, g * 5 + v, :], start=(bi == 0), stop=(bi == len(js) - 1)),
                        waits=need + bank_free[6] + [c_tok] if first else (),
                        inc=(g == 3 and bi == len(js) - 1))
                    first = False
            td_ = P.op("act", lambda e: e.activation(out=DT[:].rearrange("p g t -> p (g t)"), in_=PS[6][:], func=AF.Copy),
                       waits=[tpd])
            bank_free[6] = [td_]
            mx[(i, "d")] = td_

        def Py(i):
            bs = i % 2
            td_ = mx[(i, "d")]
            tpy = None
            for g in range(4):
                tpy = P.op("pe", lambda e, g=g: e.matmul(
                    out=PS[6][:, g * 128:(g + 1) * 128], lhsT=WPOOL[:, g, :], rhs=DT[:, g, :], start=True, stop=True),
                    waits=[td_, late["wB"]] + bank_free[6] if g == 0 else (), inc=(g == 3))
            ty = P.op("dve", lambda e: e.tensor_tensor(out=MIXT[:, bs, 4:8, :],
                                                       in0=PS[6][:].rearrange("p (g t) -> p g t", g=4),
                                                       in1=bc(PSCALE[:], 2, 128), op=ALU.mult), waits=[tpy, c_tok])
            bank_free[6] = [ty]
            mx[(i, "y")] = ty

        def Op(i):
            s = i % HS
            bs = i % 2
            for nh, bk in ((0, 5), (1, 6)):
                w0 = [mx[(i, "n", 0)], mx[(i, "n", 1)], mx[(i, "y")], late["wB"]] + bank_free[bk]
                to = None
                for c in range(8):
                    to = P.op("pe", lambda e, c=c, bk=bk, nh=nh: e.matmul(
                        out=PS[bk][:], lhsT=MIXT[:, bs, c, :], rhs=WOUT[:, c, nh * 512:(nh + 1) * 512],
                        start=(c == 0), stop=(c == 7)), waits=w0 if c == 0 else (), inc=(c == 7))
                te = P.op("dve", lambda e, bk=bk, nh=nh: e.tensor_tensor(
                    out=H[:, s, nh * 512:(nh + 1) * 512], in0=PS[bk][:], in1=H[:, s, nh * 512:(nh + 1) * 512],
                    op=ALU.add), waits=[to, h_ready[i]])
                bank_free[bk] = [te]
                h_ready[i] = te

        RING_N = (XN, "xn", 2, 0)
        RING_C = (XN, "xc", 2, 2)
        RING_A = (XN, "xa", 4, 0)
        xc_free = xn_free

        def stage_b_pro(b_tiles, w_list, na, done):
            pro = [t for t in w_list if t <= b_tiles[0] + 1]
            for t in pro:
                if t in done or t not in h_ready:
                    continue
                done.add(t)
                na[t] = Na(t, RING_N, xn_free)
                tn = Nt(na[t], 1, XNTB[:, t % 2, :, :])
                Wp(t, tn)
                QKT(t)

        def stage_b(p, b_tiles, w_list, na, done):
            xt_c = {}
            ca = {}
            wl = list(w_list)
            stage_b_pro(b_tiles, wl, na, done)
            rest = [t for t in wl if t > b_tiles[0] + 1]
            if rest:
                na[rest[0]] = Na(rest[0], RING_N, xn_free)
            for k in b_tiles:
                sqn = sqc = None
                if (k + 3) in rest:
                    sqn = Na_sq(k + 3, "dve")
                if k - 1 >= b_tiles[0]:
                    sqc = Na_sq(k - 1, "dve")
                Sc2(k)
                Pd(k)
                tn = None
                if (k + 2) in rest:
                    tn = Nt(na[k + 2], 1, XNTB[:, (k + 2) % 2, :, :])
                Vp(k, 0)
                Vp(k, 1)
                Py(k)
                Vn(k, 0)
                Vn(k, 1)
                if sqn is not None:
                    na[k + 3] = Na_rest(k + 3, RING_N, xn_free, sqn, copy_eng="dve")
                if sqc is not None:
                    ca[k - 1] = Na_rest(k - 1, RING_C, xc_free, sqc, copy_eng="dve")
                if (k + 2) in rest:
                    Wp(k + 2, tn)
                Op(k)
                if (k + 2) in rest:
                    QKT(k + 2)
                if k - 1 >= b_tiles[0]:
                    li = k - 1 - b_tiles[0]
                    xt_c[k - 1] = Nt(ca[k - 1], 2, XNT[:, :, li * 128:(li + 1) * 128], bank=2)
            kl = b_tiles[-1]
            ca[kl] = Na(kl, RING_C, xc_free)
            li = kl - b_tiles[0]
            xt_c[kl] = Nt(ca[kl], 2, XNT[:, :, li * 128:(li + 1) * 128])
            return [xt_c[t] for t in b_tiles]

        def final(t):
            s = t % HS
            col = state["st"] * 4
            state["st"] += 1
            t1 = P.op("act", lambda e: e.activation(out=JUNK[:], in_=H[:, s, :], func=AF.Square,
                                                    accum_out=ST[:, col:col + 1]), waits=[h_ready[t], t_init])
            t3 = P.op("act", lambda e: e.activation(out=ST[:, col + 2:col + 3], in_=ST[:, col:col + 1], func=AF.Ln,
                                                    scale=1.0 / D, bias=EPS[:, 0:1]), waits=[t1])
            t4 = P.op("act", lambda e: e.activation(out=ST[:, col + 3:col + 4], in_=ST[:, col + 2:col + 3],
                                                    func=AF.Exp, scale=-0.5), waits=[t3])
            t5 = P.op("dve", lambda e: e.scalar_tensor_tensor(out=H[:, s, :], in0=H[:, s, :],
                                                              scalar=ST[:, col + 3:col + 4], in1=GFIN[:],
                                                              op0=ALU.mult, op1=ALU.mult), waits=[t4, c_tok])
            store_tok[t] = P.dma("pool", out_d[t * 128:(t + 1) * 128, :], H[:, s, :], f"o{s}", waits=[t5])

        a_lists = []
        nxt = 0
        for p in range(NPASS):
            a_hi = min(NT_ALL, (p + 1) * TPP + 1)
            a_lists.append(list(range(nxt, a_hi)))
            nxt = a_hi
        loaded = set()

        def ensure_load(t):
            if t not in loaded and t < NT_ALL:
                loaded.add(t)
                load_x(t)

        xnt_free = None
        na_prev = {}
        pending_fin = []
        for p in range(NPASS):
            b_tiles = list(range(p * TPP, (p + 1) * TPP))
            a_tiles = a_lists[p]
            next_a = a_lists[p + 1] if p + 1 < NPASS else []
            for t in a_tiles:
                ensure_load(t)
            na = dict(na_prev)
            for t in a_tiles[:4]:
                if t not in na:
                    na[t] = Na(t, RING_A, xn_free, sq_eng=("dve" if t % 2 == 0 else "act"))
            xt = []
            if xnt_free is not None:
                P.wait_only("dve", [xnt_free])
            for li, t in enumerate(a_tiles):
                xt.append(Nt(na[t], 0, XNT[:, :, li * 128:(li + 1) * 128], bank=("T" if li % 2 == 0 else 2)))
                if li + 4 < len(a_tiles):
                    t4_ = a_tiles[li + 4]
                    na[t4_] = Na(t4_, RING_A, xn_free, sq_eng=("dve" if t4_ % 2 == 0 else "act"))
                if li == 3 and pending_fin:
                    pending_fin.pop()()
            if pending_fin:
                pending_fin.pop()()
            if p == 0:
                late_sp()
            nb_ = {}
            done_ = set()
            npro = len([t for t in a_tiles if t <= b_tiles[0] + 1])

            sched = {}
            pro_tiles = [t for t in a_tiles if t <= b_tiles[0] + 1]
            tn_ = {}
            for j, t in enumerate(pro_tiles):
                base = len(pro_tiles) + j

                def s_na(t=t, nb_=nb_, done_=done_):
                    done_.add(t)
                    nb_[t] = Na(t, RING_N, xn_free)

                def s_nt(t=t, nb_=nb_, tn_=tn_):
                    tn_[t] = Nt(nb_[t], 1, XNTB[:, t % 2, :, :])

                def s_w(t=t, tn_=tn_):
                    Wp(t, tn_[t])

                def s_q(t=t):
                    QKT(t)
                sched.setdefault(base, []).append(s_na)
                sched.setdefault(base + 1, []).append(s_nt)
                sched.setdefault(base + 2, []).append(s_w)
                sched.setdefault(base + 4, []).append(s_q)
            last_li = len(a_tiles) - 1
            pending = []

            def cb1(li, sched=sched, last_li=last_li):
                for k_ in sorted(sched):
                    if k_ <= li or li == last_li:
                        for f_ in sched.pop(k_):
                            f_()

            xnt_free = ffn(0, a_tiles, xt, on_last_tile=cb1)
            P.wait_only("dve", [xnt_free])
            xt = stage_b(p, b_tiles, a_tiles, nb_, done_)
            if next_a:
                ensure_load(next_a[0])

            def fin(t):
                final(t)
                if (t + HS) in next_a:
                    ensure_load(t + HS)

            na_next = {}

            def cb(li, b_tiles=b_tiles, next_a=next_a, na_next=na_next):
                if li >= 1:
                    fin(b_tiles[li - 1])
                if li >= 4 and li - 4 < len(next_a):
                    tn_ = next_a[li - 4]
                    ensure_load(tn_)
                    na_next[tn_] = Na(tn_, RING_A, xn_free, sq_eng=("dve" if tn_ % 2 == 0 else "act"))

            xnt_free = ffn(1, b_tiles, xt, on_last_tile=cb)
            if p + 1 < NPASS:
                pending_fin.append(lambda fin=fin, t=b_tiles[-1]: fin(t))
            else:
                fin(b_tiles[-1])
            na_prev = na_next
        P.wait_only("sp", [store_tok[t] for t in range(NT_ALL)])

        block = E(nc.Block())

        @block.tensor
        def _(e):
            P.emit("pe", e, sems)

        @block.scalar
        def _(e):
            P.emit("act", e, sems)

        @block.vector
        def _(e):
            P.emit("dve", e, sems)

        @block.gpsimd
        def _(e):
            P.emit("pool", e, sems)

        @block.sync
        def _(e):
            P.emit("sp", e, sems)
    return nc


def _pool_consts(S=S):
    NT_ALL = S // 128
    wins = (2, 4, 8, 16)
    B = np.zeros((128, 20, 128), np.float64)
    for g, w in enumerate(wins):
        half = w // 2
        for (tile_i, variants) in ((0, {0: 3, 1: 2}), (1, {0: 0, 1: 1, 2: 2}), (NT_ALL - 1, {NT_ALL - 2: 0, NT_ALL - 1: 4})):
            for tl in range(128):
                t = tile_i * 128 + tl
                coef = {}
                for (lo, hi) in ((t - half, t + half - 1), (t - half + 1, t + half)):
                    a = min(max(lo, 0), S)
                    b = min(max(hi + 1, 0), S)
                    for tp in range(a, b):
                        coef[tp] = coef.get(tp, 0.0) + 0.5 / (b - a)
                coef[t] = coef.get(t, 0.0) - 1.0
                for tp, cval in coef.items():
                    j = tp // 128
                    if j not in variants:
                        continue
                    B[tp % 128, g * 5 + variants[j], tl] = cval
    return B.astype(np.float32)


def _consts(S=S):
    NT_ALL = S // 128
    ident = np.eye(128, dtype=np.float32)
    b = np.arange(128)[:, None]
    a = np.arange(128)[None, :]
    m0 = np.where(a > b, NEG, 0.0).astype(np.float32)
    m1 = np.where(b > a, NEG, 0.0).astype(np.float32)
    negm = np.stack([np.tile(m0, (1, 4)), np.tile(m1, (1, 4))], axis=1)
    inv_freq = np.float32(500000.0) ** (-(np.arange(0, 16, 2, dtype=np.float32) / np.float32(16)))
    ang = np.arange(S, dtype=np.float32)[:, None] * inv_freq[None, :].astype(np.float32)
    emb = np.concatenate([ang, ang], axis=-1).astype(np.float32)
    cos = np.cos(emb.astype(np.float64)).astype(np.float32)
    sin = np.sin(emb.astype(np.float64)).astype(np.float32)
    sin_s = sin.copy()
    sin_s[:, 0:8] = -sin[:, 0:8]
    cosT = np.ascontiguousarray(cos.reshape(NT_ALL, 128, 16).transpose(1, 0, 2))
    sinT = np.ascontiguousarray(sin_s.reshape(NT_ALL, 128, 16).transpose(1, 0, 2))
    return ident, np.ascontiguousarray(negm), _pool_consts(S), cosT, sinT


_CACHE = {}


def prep_shared(ffn1_norm, ffn1_w_gate, ffn1_w_up, ffn1_w_down, mix_norm, w_in,
                sink_logits, pool_w, pool_scale, w_out, ffn2_norm, ffn2_w_gate,
                ffn2_w_up, ffn2_w_down, final_norm, S_=S):
    f = lambda a: np.ascontiguousarray(np.asarray(a, dtype=np.float32))
    ident, negm, bpool, cosT, sinT = _consts(S_)

    def gu_layout(wg, wu):
        arr = np.stack([f(wg)[0], f(wu)[0]])
        arr = arr.reshape(2, 8, 128, NCH, 128)
        arr = arr.transpose(3, 2, 0, 1, 4)
        return np.ascontiguousarray(arr).reshape(NCH, 128, 2 * 8 * 128)

    perm_q = []
    for c in range(4):
        perm_q += list(range(c * 64, (c + 1) * 64)) + list(range((4 + c) * 64, (5 + c) * 64))
    perm_q = np.array(perm_q)
    col_perm = np.concatenate([perm_q, np.arange(512, 1280)])
    row_perm = np.concatenate([perm_q, np.arange(512, 1024)])
    win_p = np.ascontiguousarray(f(w_in)[0][:, col_perm])
    wout_p = np.ascontiguousarray(f(w_out)[0][row_perm, :])
    wpool = np.ascontiguousarray(f(pool_w)[0].transpose(1, 0, 2))
    gs = np.stack([f(ffn1_norm)[0], f(mix_norm)[0], f(ffn2_norm)[0]])
    gfm = np.ascontiguousarray(gs.reshape(3, 8, 128).transpose(2, 0, 1))
    gfin = np.ascontiguousarray(np.broadcast_to(f(final_norm)[None, :], (128, D)))
    pscale = np.ascontiguousarray(f(pool_scale)[0].reshape(4, 128).T)
    sk = f(sink_logits)[0]
    sinkb = np.ascontiguousarray(np.concatenate([np.broadcast_to(sk[0:4][None, :], (64, 4)),
                                                 np.broadcast_to(sk[4:8][None, :], (64, 4))], axis=0))
    return {
        "wgu1": gu_layout(ffn1_w_gate, ffn1_w_up), "wgu2": gu_layout(ffn2_w_gate, ffn2_w_up),
        "wd1": f(ffn1_w_down)[0], "wd2": f(ffn2_w_down)[0],
        "win": win_p, "wout": wout_p, "wpool": wpool, "ident": ident, "negm": negm, "bpool": bpool,
        "cosT": cosT, "sinT": sinT, "gfm": gfm, "gfin": gfin, "pscale": pscale, "sinkb": sinkb,
    }


def kernel(x, **params):
    x = np.ascontiguousarray(np.asarray(x, dtype=np.float32))
    if "nc" not in _CACHE:
        _CACHE["nc"] = build_program()
    nc = _CACHE["nc"]
    shared = prep_shared(**params)
    in_maps = []
    for c in range(8):
        m = dict(shared)
        m["x"] = x[c]
        in_maps.append(m)
    res = run_bass_kernel_spmd(nc, in_maps, core_ids=list(range(8)))
    return np.stack([np.asarray(r["out"], dtype=np.float32) for r in res.results], axis=0)
```
